# Optimizing a Trainium2 kernel written in Bass

```python
import jax, jax.numpy as jnp
from jax import lax
import numpy as np

D_MODEL = 1024
BATCH = 8
SEQ = 4096
DEPTH = 2

CHUNK = 64
Q_BLOCK = 128
HEAD_DIM = 64
CONV_WIDTH = D_MODEL // 2
LRU_WIDTH = D_MODEL // 2
CONV_GROUPS = CONV_WIDTH // HEAD_DIM
LRU_HEADS = LRU_WIDTH // HEAD_DIM
SHORT_CONV_K = 3
LRU_CONV_K = 4
RG_C = 8.0
SB_HEADS = D_MODEL // HEAD_DIM
D_FF = ((8 * D_MODEL // 3 + 255) // 256) * 256
N_MOD = 6
EPS = 1e-6

kernel_name = 'hybrid_shortconv_rglru_stickbreaking_adaln'


def rms_norm(x, g):
    x32 = x.astype(jnp.float32)
    y = x32 * lax.rsqrt(jnp.mean(x32 * x32, axis=-1, keepdims=True) + EPS)
    return (y * g.astype(jnp.float32)).astype(x.dtype)


def ada_modulation(c, ada_w, ada_b):
    m = jax.nn.silu(c) @ ada_w + ada_b
    return jnp.split(m, N_MOD, axis=-1)


def modulate(h, shift, scale):
    return h * (1.0 + scale[:, None, :]) + shift[:, None, :]


def causal_depthwise_conv(x, w):
    k_width = w.shape[0]
    seq = x.shape[1]
    xp = jnp.pad(x, ((0, 0), (k_width - 1, 0), (0, 0)))
    y = xp[:, 0:seq] * w[0]
    for k in range(1, k_width):
        y = y + xp[:, k:k + seq] * w[k]
    return y


def rg_lru(xr, w_a, b_a, w_x, b_x, lam):
    bsz, seq, width = xr.shape
    n_blk, blk = w_a.shape[0], w_a.shape[1]
    x32 = xr.astype(jnp.float32)
    xh = x32.reshape(bsz, seq, n_blk, blk)
    r = jax.nn.sigmoid(jnp.einsum('bshi,hij->bshj', xh, w_a.astype(jnp.float32)) + b_a).reshape(bsz, seq, width)
    i = jax.nn.sigmoid(jnp.einsum('bshi,hij->bshj', xh, w_x.astype(jnp.float32)) + b_x).reshape(bsz, seq, width)
    log_a = -RG_C * r * jax.nn.softplus(-lam.astype(jnp.float32))
    a = jnp.exp(log_a)
    b = jnp.sqrt(-jnp.expm1(2.0 * log_a)) * (i * x32)
    n_chunks = seq // CHUNK
    a_c = a.reshape(bsz, n_chunks, CHUNK, width).transpose(1, 0, 2, 3)
    b_c = b.reshape(bsz, n_chunks, CHUNK, width).transpose(1, 0, 2, 3)

    def combine(lhs, rhs):
        al, bl = lhs
        ar, br = rhs
        return al * ar, ar * bl + br

    def chunk_step(h0, ab):
        a_k, b_k = ab
        a_cum, b_cum = lax.associative_scan(combine, (a_k, b_k), axis=1)
        h = a_cum * h0[:, None, :] + b_cum
        return h[:, -1], h

    h_init = jnp.zeros((bsz, width), jnp.float32)
    _, hs = lax.scan(chunk_step, h_init, (a_c, b_c))
    return hs.transpose(1, 0, 2, 3).reshape(bsz, seq, width)


def stick_breaking_attention(q, k, v):
    seq, dh = q.shape[2], q.shape[3]
    scale = dh ** -0.5
    outs = []
    for blk in range(seq // Q_BLOCK):
        q0, q1 = blk * Q_BLOCK, (blk + 1) * Q_BLOCK
        qb = q[:, :, q0:q1].astype(jnp.float32)
        kb = k[:, :, :q1].astype(jnp.float32)
        vb = v[:, :, :q1].astype(jnp.float32)
        z = jnp.einsum('bhqd,bhkd->bhqk', qb, kb) * scale
        q_pos = jnp.arange(q0, q1)[:, None]
        k_pos = jnp.arange(q1)[None, :]
        strict = k_pos < q_pos
        log_keep = jnp.where(strict, jax.nn.log_sigmoid(-z), 0.0)
        prefix = jnp.cumsum(log_keep, axis=-1)
        after = prefix[..., -1:] - prefix
        w = jnp.where(strict, jnp.exp(jax.nn.log_sigmoid(z) + after), 0.0)
        outs.append(jnp.einsum('bhqk,bhkd->bhqd', w, vb))
    return jnp.concatenate(outs, axis=2)


def swiglu(h, w_gate, w_up, w_down):
    return (jax.nn.silu(h @ w_gate) * (h @ w_up)) @ w_down


def even_layer(x, c, ada_w, ada_b, mix_norm, w_in, conv_a_w, conv_b_w, conv_b_b,
               rg_a_w, rg_a_b, rg_x_w, rg_x_b, rg_lambda, w_out,
               ffn_norm, ffn_w_gate, ffn_w_up, ffn_w_down):
    sh_m, sc_m, g_m, sh_f, sc_f, g_f = ada_modulation(c, ada_w, ada_b)
    h = modulate(rms_norm(x, mix_norm), sh_m, sc_m)
    u = h @ w_in
    cuts = [CONV_WIDTH, 2 * CONV_WIDTH, 3 * CONV_WIDTH, 3 * CONV_WIDTH + LRU_WIDTH]
    a_b, a_c, a_x, r_gate, r_x = jnp.split(u, cuts, axis=-1)
    y_a = a_b * causal_depthwise_conv(a_c * a_x, conv_a_w)
    xr = causal_depthwise_conv(r_x, conv_b_w) + conv_b_b
    y_b = jax.nn.gelu(r_gate) * rg_lru(xr, rg_a_w, rg_a_b, rg_x_w, rg_x_b, rg_lambda).astype(x.dtype)
    y = jnp.concatenate([y_a, y_b], axis=-1) @ w_out
    x = x + g_m[:, None, :] * y
    hf = modulate(rms_norm(x, ffn_norm), sh_f, sc_f)
    return x + g_f[:, None, :] * swiglu(hf, ffn_w_gate, ffn_w_up, ffn_w_down)


def odd_layer(x, c, ada_w, ada_b, mix_norm, w_qkv, q_norm, k_norm, w_out,
              ffn_norm, ffn_w_gate, ffn_w_up, ffn_w_down):
    sh_m, sc_m, g_m, sh_f, sc_f, g_f = ada_modulation(c, ada_w, ada_b)
    h = modulate(rms_norm(x, mix_norm), sh_m, sc_m)
    bsz, seq, _ = x.shape
    qkv = (h @ w_qkv).reshape(bsz, seq, 3, SB_HEADS, HEAD_DIM)
    q = rms_norm(qkv[:, :, 0], q_norm).transpose(0, 2, 1, 3)
    k = rms_norm(qkv[:, :, 1], k_norm).transpose(0, 2, 1, 3)
    v = qkv[:, :, 2].transpose(0, 2, 1, 3)
    o = stick_breaking_attention(q, k, v).astype(x.dtype)
    o = o.transpose(0, 2, 1, 3).reshape(bsz, seq, SB_HEADS * HEAD_DIM)
    x = x + g_m[:, None, :] * (o @ w_out)
    hf = modulate(rms_norm(x, ffn_norm), sh_f, sc_f)
    return x + g_f[:, None, :] * swiglu(hf, ffn_w_gate, ffn_w_up, ffn_w_down)


def setup_inputs(seed: int = 0) -> dict:
    key = jax.random.key(seed)
    ks = jax.random.split(key, 40)
    f32 = jnp.float32

    def nrm(k, shape, scale):
        return jax.random.normal(k, shape, f32) * scale

    d = D_MODEL
    u = jax.random.uniform(ks[12], (LRU_WIDTH,), f32, 0.9, 0.999)
    a0 = u ** (1.0 / RG_C)
    lam = jnp.log(a0) - jnp.log1p(-a0)
    return {
        'x': nrm(ks[0], (BATCH, SEQ, d), 1.0),
        'c': nrm(ks[1], (BATCH, d), 1.0),
        'l0_ada_w': nrm(ks[2], (d, N_MOD * d), 0.5 * d ** -0.5),
        'l0_ada_b': nrm(ks[3], (N_MOD * d,), 0.02),
        'l0_mix_norm': 1.0 + nrm(ks[4], (d,), 0.02),
        'l0_w_in': nrm(ks[5], (d, 3 * CONV_WIDTH + 2 * LRU_WIDTH), d ** -0.5),
        'l0_conv_a_w': nrm(ks[6], (SHORT_CONV_K, CONV_WIDTH), SHORT_CONV_K ** -0.5),
        'l0_conv_b_w': nrm(ks[7], (LRU_CONV_K, LRU_WIDTH), LRU_CONV_K ** -0.5),
        'l0_conv_b_b': nrm(ks[8], (LRU_WIDTH,), 0.01),
        'l0_rg_a_w': nrm(ks[9], (LRU_HEADS, HEAD_DIM, HEAD_DIM), HEAD_DIM ** -0.5),
        'l0_rg_a_b': nrm(ks[10], (LRU_HEADS, HEAD_DIM), 0.01),
        'l0_rg_x_w': nrm(ks[11], (LRU_HEADS, HEAD_DIM, HEAD_DIM), HEAD_DIM ** -0.5),
        'l0_rg_x_b': nrm(ks[13], (LRU_HEADS, HEAD_DIM), 0.01),
        'l0_rg_lambda': lam,
        'l0_w_out': nrm(ks[14], (CONV_WIDTH + LRU_WIDTH, d), (CONV_WIDTH + LRU_WIDTH) ** -0.5),
        'l0_ffn_norm': 1.0 + nrm(ks[15], (d,), 0.02),
        'l0_ffn_w_gate': nrm(ks[16], (d, D_FF), d ** -0.5),
        'l0_ffn_w_up': nrm(ks[17], (d, D_FF), d ** -0.5),
        'l0_ffn_w_down': nrm(ks[18], (D_FF, d), D_FF ** -0.5),
        'l1_ada_w': nrm(ks[19], (d, N_MOD * d), 0.5 * d ** -0.5),
        'l1_ada_b': nrm(ks[20], (N_MOD * d,), 0.02),
        'l1_mix_norm': 1.0 + nrm(ks[21], (d,), 0.02),
        'l1_w_qkv': nrm(ks[22], (d, 3 * SB_HEADS * HEAD_DIM), d ** -0.5),
        'l1_q_norm': 1.0 + nrm(ks[23], (HEAD_DIM,), 0.02),
        'l1_k_norm': 1.0 + nrm(ks[24], (HEAD_DIM,), 0.02),
        'l1_w_out': nrm(ks[25], (SB_HEADS * HEAD_DIM, d), (SB_HEADS * HEAD_DIM) ** -0.5),
        'l1_ffn_norm': 1.0 + nrm(ks[26], (d,), 0.02),
        'l1_ffn_w_gate': nrm(ks[27], (d, D_FF), d ** -0.5),
        'l1_ffn_w_up': nrm(ks[28], (d, D_FF), d ** -0.5),
        'l1_ffn_w_down': nrm(ks[29], (D_FF, d), D_FF ** -0.5),
    }


def reference(x, c,
              l0_ada_w, l0_ada_b, l0_mix_norm, l0_w_in, l0_conv_a_w, l0_conv_b_w, l0_conv_b_b,
              l0_rg_a_w, l0_rg_a_b, l0_rg_x_w, l0_rg_x_b, l0_rg_lambda, l0_w_out,
              l0_ffn_norm, l0_ffn_w_gate, l0_ffn_w_up, l0_ffn_w_down,
              l1_ada_w, l1_ada_b, l1_mix_norm, l1_w_qkv, l1_q_norm, l1_k_norm, l1_w_out,
              l1_ffn_norm, l1_ffn_w_gate, l1_ffn_w_up, l1_ffn_w_down):
    even_params = [(l0_ada_w, l0_ada_b, l0_mix_norm, l0_w_in, l0_conv_a_w, l0_conv_b_w, l0_conv_b_b,
                    l0_rg_a_w, l0_rg_a_b, l0_rg_x_w, l0_rg_x_b, l0_rg_lambda, l0_w_out,
                    l0_ffn_norm, l0_ffn_w_gate, l0_ffn_w_up, l0_ffn_w_down)]
    odd_params = [(l1_ada_w, l1_ada_b, l1_mix_norm, l1_w_qkv, l1_q_norm, l1_k_norm, l1_w_out,
                   l1_ffn_norm, l1_ffn_w_gate, l1_ffn_w_up, l1_ffn_w_down)]
    for layer in range(DEPTH):
        if layer % 2 == 0:
            x = even_layer(x, c, *even_params[layer // 2])
        else:
            x = odd_layer(x, c, *odd_params[layer // 2])
    return x
```

```python
import numpy as np
import concourse.bass as bass
import concourse.mybir as mybir
from concourse.bass_utils import run_bass_kernel_spmd

F32 = mybir.dt.float32
BF16 = mybir.dt.bfloat16
AF = mybir.ActivationFunctionType
ALU = mybir.AluOpType

SEM_ROLL = 30000
NCORES = 8
SEQ = 4096
D = 1024
T = 512
NT = SEQ // T
DFF = 2816
NF = DFF // 128
EPS = 1e-6


class Buf:
    __slots__ = ("ap", "last_w", "readers", "name")

    def __init__(self, ap, name=""):
        self.ap = ap
        self.last_w = None
        self.readers = {}
        self.name = name

    def __getitem__(self, idx):
        return self.ap[idx]


class _Eng:
    def __init__(self, nc, name, eng):
        self.name = name
        self.eng = eng
        self.sem = nc.alloc_semaphore(f"s_{name}_0")
        self.nsem = 1
        self.count = 0
        self.seen = {}
        self.deferred = []


class Sched:
    def __init__(self, nc, n_dma_sems=8):
        self.nc = nc
        self.E = {
            "pe": _Eng(nc, "pe", nc.tensor),
            "act": _Eng(nc, "act", nc.scalar),
            "dve": _Eng(nc, "dve", nc.vector),
            "pool": _Eng(nc, "pool", nc.gpsimd),
            "sp": _Eng(nc, "sp", nc.sync),
        }
        self.dma_sems = {}
        self.n_dma_sems = n_dma_sems
        self.all_tokens = {}
        self.n_inst = 0

    def _deps(self, reads, writes):
        need = {}

        def add(tok):
            if tok is None:
                return
            k = id(tok[0])
            if k not in need or need[k][1] < tok[1]:
                need[k] = tok
        for b in reads:
            add(b.last_w)
        for b in writes:
            add(b.last_w)
            for t in b.readers.values():
                add(t)
        return need

    def _wait(self, E, need, skip_self=False):
        for k, (sem, val) in need.items():
            if skip_self and sem is E.sem:
                continue
            if E.seen.get(k, 0) < val:
                E.eng.wait_ge(sem, val)
                E.seen[k] = val

    def _record(self, tok, reads, writes):
        k = id(tok[0])
        for b in reads:
            b.readers[k] = tok
        for b in writes:
            b.last_w = tok
            b.readers = {}
        self.all_tokens[k] = tok

    def op(self, en, fn, reads=(), writes=(), inc=True):
        E = self.E[en]
        if E.count >= SEM_ROLL:
            E.sem = self.nc.alloc_semaphore(f"s_{E.name}_{E.nsem}")
            E.nsem += 1
            E.count = 0
        need = self._deps(reads, writes)
        self._wait(E, need, skip_self=(en == "pe"))
        ins = fn(E.eng)
        self.n_inst += 1
        if not inc:
            E.deferred.append((tuple(reads), tuple(writes)))
            return ins
        E.count += 1
        ins.then_inc(E.sem, 1)
        tok = (E.sem, E.count)
        for (r_, w_) in E.deferred:
            self._record(tok, r_, w_)
        E.deferred = []
        self._record(tok, reads, writes)
        return ins

    def dma(self, qn, out_ap, in_ap, reads=(), writes=(), **kw):
        E = self.E[qn]
        if qn not in self.dma_sems:
            self.dma_sems[qn] = [[self.nc.alloc_semaphore(f"d_{qn}_{i}"), 0]
                                 for i in range(self.n_dma_sems)]
            self.dma_sems[qn + "_rr"] = 0
        rr = self.dma_sems[qn + "_rr"]
        slot = self.dma_sems[qn][rr % self.n_dma_sems]
        self.dma_sems[qn + "_rr"] = rr + 1
        sem, cnt = slot
        if cnt >= SEM_ROLL:
            if E.seen.get(id(sem), 0) < cnt:
                E.eng.wait_ge(sem, cnt)
            sem = self.nc.alloc_semaphore(f"d_{qn}_r{rr}")
            cnt = 0
            slot[0] = sem
        need = self._deps(reads, writes)
        if cnt > 0:
            k = id(sem)
            if k not in need or need[k][1] < cnt:
                need[k] = (sem, cnt)
        self._wait(E, need)
        ins = E.eng.dma_start(out=out_ap, in_=in_ap, **kw)
        cnt += 16
        slot[1] = cnt
        ins.then_inc(sem, 16)
        self._record((sem, cnt), reads, writes)
        self.n_inst += 1
        return ins

    def barrier(self):
        for E in self.E.values():
            for k, (sem, val) in self.all_tokens.items():
                if E.seen.get(k, 0) < val:
                    E.eng.wait_ge(sem, val)
                    E.seen[k] = val


WSPEC = [
    ("w_in", 5, 8 * 512),
    ("w_out0", 2, 8 * 512),
    ("w_gu0", 11, 8 * 512),
    ("w_dn0", 8, NF * 128),
    ("w_qkv", 6, 8 * 512),
    ("w_out1", 2, 8 * 512),
    ("w_gu1", 11, 8 * 512),
    ("w_dn1", 8, NF * 128),
]
WMAX = 8 * 512
ADA_SL = 12
ADA_W = 512


def build_nc(do_l0=True, do_l1=True, ntiles=NT):
    nc = bass.Bass("TRN2", target_bir_lowering=False)
    S = Sched(nc)

    def din(name, shape, dt=F32):
        return nc.dram_tensor(name, list(shape), dt, kind="ExternalInput").ap()

    x_d = din("x", [SEQ, D])
    y_d = nc.dram_tensor("y", [SEQ, D], F32, kind="ExternalOutput").ap()
    cT_d = din("cT", [128, 8])
    ident_d = din("ident", [128, 128])
    cst_d = din("cst", [128, 3, 128])
    bd_d = din("bd", [128, 8, 128])
    vec0_d = din("vec0", [128, 60])
    vec1_d = din("vec1", [128, 18])
    adaw_d = [din(f"adaw{l}", [ADA_SL, 128, 8 * ADA_W]) for l in range(2)]
    adab_d = [din(f"adab{l}", [128, 48]) for l in range(2)]
    w_d = {n: din(n, [ns, 128, sz]) for (n, ns, sz) in WSPEC}
    ws_d = {n: nc.dram_tensor(n + "_bf", [ns, 128, sz], BF16, kind="Internal").ap() for (n, ns, sz) in WSPEC}
    ws_buf = {n: [Buf(None, f"{n}_s{i}") for i in range(ns)] for (n, ns, sz) in WSPEC}
    kscr = nc.dram_tensor("kscr", [8, 128, SEQ], BF16, kind="Internal").ap()
    vscr = nc.dram_tensor("vscr", [8, 128, SEQ // 128, 128], BF16, kind="Internal").ap()
    kscr_b = [Buf(None, f"kscr{c}") for c in range(8)]
    vscr_b = Buf(None, "vscr")

    def sb(name, shape, dt=F32):
        return Buf(nc.alloc_sbuf_tensor("sb_" + name, list(shape), dt).ap(), name)

    ident = sb("ident", [128, 128])
    ones_b = sb("ones_b", [128, 128], BF16)
    negones_b = sb("negones_b", [128, 128], BF16)
    negL_b = sb("negL_b", [128, 128], BF16)
    mask_b = sb("mask_b", [128, 128], BF16)
    ident_b = sb("ident_b", [128, 128], BF16)
    bdones_b = sb("bdones_b", [128, 128], BF16)
    bdw = sb("bdw", [128, 8, 128], BF16)
    vec0 = sb("vec0", [128, 60])
    vec1 = sb("vec1", [128, 18])
    mod = [sb(f"mod{l}", [128, 48]) for l in range(2)]
    der = sb("der", [128, 64])
    P2t = [nc.alloc_psum_tensor(f"pp{i}", [128, 1024], F32).ap() for i in range(4)]
    PS = [Buf(P2t[i // 2][:, (i % 2) * 512:(i % 2 + 1) * 512], f"ps{i}") for i in range(8)]
    ps_rr = [0]

    def ps_get():
        b = PS[ps_rr[0] % 7]
        ps_rr[0] += 1
        return b
    SS = PS[7]

    NSTG = 3
    with nc.sbuf_tensor("stg32", [128, NSTG, WMAX], F32) as stg32_t, \
            nc.sbuf_tensor("stg16", [128, NSTG, WMAX], BF16) as stg16_t, \
            nc.sbuf_tensor("cst32", [128, 3, 128], F32) as cst32_t, \
            nc.sbuf_tensor("bd32", [128, 8, 128], F32) as bd32_t, \
            nc.sbuf_tensor("ctile", [128, 16], F32) as ctile_t, \
            nc.sbuf_tensor("mrow", [1, 48 * 128], F32) as mrow_t, \
            nc.sbuf_tensor("adab_s", [128, 48], F32) as adab_t:
        stg32 = [Buf(stg32_t.ap()[:, i, :], f"stg32_{i}") for i in range(NSTG)]
        stg16 = [Buf(stg16_t.ap()[:, i, :], f"stg16_{i}") for i in range(NSTG)]
        cst32 = Buf(cst32_t.ap())
        bd32 = Buf(bd32_t.ap())
        ctile = Buf(ctile_t.ap())
        mrow = Buf(mrow_t.ap())
        adab_s = Buf(adab_t.ap())

        S.dma("sp", ident[:], ident_d, writes=[ident])
        S.dma("sp", cst32[:], cst_d, writes=[cst32])
        S.dma("sp", bd32[:], bd_d, writes=[bd32])
        S.dma("sp", vec0[:], vec0_d, writes=[vec0])
        S.dma("sp", vec1[:], vec1_d, writes=[vec1])
        S.dma("sp", ctile[:, 0:8], cT_d, writes=[ctile])
        S.op("pool", lambda e: e.memset(ones_b[:], 1.0), [], [ones_b])
        S.op("pool", lambda e: e.memset(negones_b[:], -1.0), [], [negones_b])
        S.op("dve", lambda e: e.tensor_copy(out=negL_b[:], in_=cst32[:, 0, :]), [cst32], [negL_b])
        S.op("dve", lambda e: e.tensor_scalar(out=mask_b[:], in0=cst32[:, 1, :], scalar1=-1.0, scalar2=240.0, op0=ALU.add, op1=ALU.mult), [cst32], [mask_b])
        S.op("dve", lambda e: e.tensor_copy(out=ident_b[:], in_=ident[:]), [ident], [ident_b])
        S.op("dve", lambda e: e.tensor_copy(out=bdones_b[:], in_=cst32[:, 2, :]), [cst32], [bdones_b])
        S.op("dve", lambda e: e.tensor_copy(out=bdw[:], in_=bd32[:]), [bd32], [bdw])
        S.op("act", lambda e: e.activation(out=ctile[:, 8:16], in_=ctile[:, 0:8], func=AF.Silu), [ctile], [ctile])
        kk = 0
        for l in range(2):
            if (l == 0 and not do_l0) or (l == 1 and not do_l1):
                continue
            for g in range(ADA_SL):
                st = stg32[kk % NSTG]
                kk += 1
                S.dma("sp", st[:, 0:8 * ADA_W], adaw_d[l][g], writes=[st])
                rp = ps_get()
                for kc in range(8):
                    S.op("pe", lambda e, st=st, kc=kc, rp=rp: e.matmul(
                        rp[0:1, 0:ADA_W], lhsT=ctile[:, 8 + kc:9 + kc], rhs=st[:, kc * ADA_W:(kc + 1) * ADA_W],
                        start=(kc == 0), stop=(kc == 7)), [st, ctile], [rp], inc=(kc == 7))
                S.op("dve", lambda e, g=g, rp=rp: e.tensor_copy(out=mrow[0:1, g * ADA_W:(g + 1) * ADA_W], in_=rp[0:1, 0:ADA_W]), [rp], [mrow])
            tp_ = ps_get()
            for j in range(48):
                S.op("pe", lambda e, j=j, tp_=tp_: e.transpose(out=tp_[:, j:j + 1], in_=mrow[0:1, j * 128:(j + 1) * 128], identity=ident[0:1, 0:1]), [mrow, ident], [tp_])
            S.dma("sp", adab_s[:], adab_d[l], writes=[adab_s])
            S.op("dve", lambda e, l=l, tp_=tp_: e.tensor_tensor(out=mod[l][:], in0=tp_[:, 0:48], in1=adab_s[:], op=ALU.add),
                 [tp_, adab_s], [mod[l]])
        S.op("dve", lambda e: e.scalar_tensor_tensor(out=der[:, 0:8], in0=mod[0][:, 8:16], scalar=1.0, in1=vec0[:, 0:8], op0=ALU.add, op1=ALU.mult), [mod[0], vec0], [der])
        S.op("dve", lambda e: e.scalar_tensor_tensor(out=der[:, 8:16], in0=mod[0][:, 32:40], scalar=1.0, in1=vec0[:, 8:16], op0=ALU.add, op1=ALU.mult), [mod[0], vec0], [der])
        S.op("dve", lambda e: e.scalar_tensor_tensor(out=der[:, 16:24], in0=mod[1][:, 8:16], scalar=1.0, in1=vec1[:, 0:8], op0=ALU.add, op1=ALU.mult), [mod[1], vec1], [der])
        S.op("dve", lambda e: e.scalar_tensor_tensor(out=der[:, 24:32], in0=mod[1][:, 32:40], scalar=1.0, in1=vec1[:, 8:16], op0=ALU.add, op1=ALU.mult), [mod[1], vec1], [der])
        S.op("act", lambda e: e.activation(out=der[:, 40:44], in_=vec0[:, 56:60], func=AF.Exp, scale=-1.0), [vec0], [der])
        S.op("act", lambda e: e.activation(out=der[:, 44:48], in_=der[:, 40:44], func=AF.Ln, bias=1.0, scale=1.0), [der], [der])
        S.op("dve", lambda e: e.tensor_scalar(out=der[:, 32:36], in0=der[:, 44:48], scalar1=-8.0, scalar2=None, op0=ALU.mult), [der], [der])
        S.op("dve", lambda e: e.tensor_scalar(out=der[:, 36:37], in0=vec1[:, 16:17], scalar1=0.125, scalar2=None, op0=ALU.mult), [vec1], [der])
        S.op("dve", lambda e: e.tensor_scalar(out=der[:, 48:56], in0=vec0[:, 48:56], scalar1=0.5, scalar2=None, op0=ALU.mult), [vec0], [der])
        S.op("dve", lambda e: e.tensor_scalar(out=der[:, 56:60], in0=der[:, 32:36], scalar1=0.5, scalar2=None, op0=ALU.mult), [der], [der])
        S.barrier()

    xT = [sb(f"xT{c}", [128, T]) for c in range(8)]
    xin = [sb(f"xin{i}", [128, D]) for i in range(2)]
    hT = [sb(f"hT{c}", [128, T], BF16) for c in range(8)]
    yT = [sb(f"yT{c}", [128, T], BF16) for c in range(8)]
    actT = [sb(f"actT{f}", [128, T], BF16) for f in range(NF)]
    qT = [[sb(f"qT{c}_{h}", [128, T], BF16) for h in range(2)] for c in range(8)]
    NFP = 10
    FPt = nc.alloc_sbuf_tensor("sb_fpool", [128, NFP, T], F32).ap()
    FP = [Buf(FPt[:, i, :], f"fp{i}") for i in range(NFP)]
    NBP = 12
    BPt = nc.alloc_sbuf_tensor("sb_bpool", [128, NBP, T], BF16).ap()
    BP = [Buf(BPt[:, i, :], f"bp{i}") for i in range(NBP)]
    rstd = sb("rstd", [128, T])
    GG = [sb(f"ggt{i}", [128, T]) for i in range(4)]
    XR = [sb(f"xr{i}", [128, T]) for i in range(4)]
    WR = [sb(f"wr{i}", [128, WMAX], BF16) for i in range(3)]
    KR = [sb(f"kr{i}", [128, SEQ], BF16) for i in range(2)]
    VR = [sb(f"vr{i}", [128, SEQ], BF16) for i in range(2)]
    Pt = [sb(f"Pt{j}", [128, 2 + T]) for j in range(4)]
    RX = [sb(f"RX{j}", [128, 3 + T]) for j in range(4)]
    hst = sb("hst", [128, 4])
    Rb = [sb(f"Rb{h}", [128, T], BF16) for h in range(2)]
    fp_rr = [0]
    bp_rr = [0]

    def f32_get():
        b = FP[fp_rr[0] % len(FP)]
        fp_rr[0] += 1
        return b

    def bf_get():
        b = BP[bp_rr[0] % len(BP)]
        bp_rr[0] += 1
        return b

    def f32_get2():
        if fp_rr[0] % 2:
            fp_rr[0] += 1
        i = fp_rr[0] % NFP
        fp_rr[0] += 2
        return FPt[:, i:i + 2, :], [FP[i], FP[i + 1]]

    def bf_get2():
        if bp_rr[0] % 2:
            bp_rr[0] += 1
        i = bp_rr[0] % NBP
        bp_rr[0] += 2
        return BPt[:, i:i + 2, :], [BP[i], BP[i + 1]]

    for j in range(4):
        S.op("pool", lambda e, j=j: e.memset(Pt[j][:, 0:2], 0.0), [], [Pt[j]])
        S.op("pool", lambda e, j=j: e.memset(RX[j][:, 0:3], 0.0), [], [RX[j]])
    S.op("pool", lambda e: e.memset(hst[:], 0.0), [], [hst])
    for c in range(8):
        for h in range(2):
            S.op("pool", lambda e, c=c, h=h: e.memset(qT[c][h][:], 0.0), [], [qT[c][h]])
    for h in range(2):
        S.op("pool", lambda e, h=h: e.memset(Rb[h][:], 0.0), [], [Rb[h]])

    seq = []
    for ti in range(ntiles):
        for (n, ns, sz) in WSPEC:
            l1w = n in ("w_qkv", "w_out1", "w_gu1", "w_dn1")
            if (l1w and not do_l1) or ((not l1w) and not do_l0):
                continue
            for s in range(ns):
                seq.append((n, s, sz))
    wst = {"pos": 0, "issued": 0}
    n_tile0 = len(seq) // ntiles
    STG = [KR[0], KR[1], VR[0], VR[1]]
    st_pending = []

    def w_issue(k):
        n, s, sz = seq[k]
        r = WR[k % 3]
        if k < n_tile0:
            h = sz // 2
            for half in range(2):
                stg = STG[(2 * k + half) % 4]
                sv = stg.ap.bitcast(F32)
                S.dma("sp", sv[:, 0:h], w_d[n][s][:, half * h:(half + 1) * h], writes=[stg])
                eng = "pool" if half == 0 else "dve"
                S.op(eng, lambda e, sv=sv, r=r, h=h, half=half: e.tensor_copy(out=r[:, half * h:(half + 1) * h], in_=sv[:, 0:h]), [stg], [r])
            st_pending.append((n, s, sz, r))
        else:
            S.dma("sp", r[:, 0:sz], ws_d[n][s], reads=[ws_buf[n][s]], writes=[r])

    def st_flush(keep):
        while len(st_pending) > keep:
            n, s, sz, r = st_pending.pop(0)
            S.dma("act", ws_d[n][s], r[:, 0:sz], reads=[r], writes=[ws_buf[n][s]])

    def wnext(expect):
        i = wst["pos"]
        wst["pos"] += 1
        assert seq[i][0] == expect, (seq[i], expect)
        while wst["issued"] < min(len(seq), i + 3):
            w_issue(wst["issued"])
            wst["issued"] += 1
            st_flush(1)
        if wst["issued"] >= n_tile0 + 1:
            st_flush(0)
        return WR[i % 3]

    ss_pending = []

    def ss_flush(keep=0):
        while len(ss_pending) > keep:
            n, sq = ss_pending.pop(0)
            S.op("pe", lambda e, n=n, sq=sq: e.matmul(SS[:], lhsT=ones_b[:], rhs=sq[:], start=(n == 0), stop=(n == 7), skip_group_check=True), [ones_b, sq], [SS])

    def sumsq(n):
        sq = bf_get()
        S.op("act", lambda e: e.activation(out=sq[:], in_=xT[n][:], func=AF.Square), [xT[n]], [sq])
        ss_pending.append((n, sq))
        ss_flush(keep=1)

    def resid(n, p, G_ap, want_ss=True):
        S.op("dve", lambda e: e.scalar_tensor_tensor(out=xT[n][:], in0=p[:], scalar=G_ap[:, n:n + 1], in1=xT[n][:], op0=ALU.mult, op1=ALU.add),
             [p, xT[n], mod[0], mod[1]], [xT[n]])
        if want_ss:
            sumsq(n)

    def norm_to_h(A_ap, B_ap):
        ss_flush(0)
        lnv = f32_get()
        S.op("act", lambda e: e.activation(out=lnv[:], in_=SS[:], func=AF.Ln, scale=1.0 / D, bias=EPS), [SS], [lnv])
        S.op("act", lambda e: e.activation(out=rstd[:], in_=lnv[:], func=AF.Exp, scale=-0.5), [lnv], [rstd])
        for c in range(8):
            tmp = f32_get()
            S.op("dve", lambda e, c=c, tmp=tmp: e.scalar_tensor_tensor(out=tmp[:], in0=xT[c][:], scalar=A_ap[:, c:c + 1], in1=rstd[:], op0=ALU.mult, op1=ALU.mult),
                 [xT[c], rstd, der], [tmp])
            S.op("act", lambda e, c=c, tmp=tmp: e.activation(out=hT[c][:], in_=tmp[:], func=AF.Identity, bias=B_ap[:, c:c + 1], scale=1.0), [tmp, mod[0], mod[1]], [hT[c]])

    def kc_outer(w, col_offs):
        banks = [ps_get() for _ in col_offs]
        for kc in range(8):
            for i, off in enumerate(col_offs):
                S.op("pe", lambda e, kc=kc, i=i, off=off: e.matmul(banks[i][:], lhsT=w[:, kc * 512 + off: kc * 512 + off + 128], rhs=hT[kc][:], start=(kc == 0), stop=(kc == 7), skip_group_check=True),
                     [w, hT[kc]], [banks[i]], inc=(kc == 7))
        return banks

    def proj_resid(wname, src, G_ap):
        for s in range(2):
            w = wnext(wname)
            for q in range(4):
                n = 4 * s + q
                p = ps_get()
                for kc in range(8):
                    S.op("pe", lambda e, w=w, q=q, kc=kc, p=p: e.matmul(p[:], lhsT=w[:, kc * 512 + q * 128: kc * 512 + (q + 1) * 128], rhs=src[kc][:], start=(kc == 0), stop=(kc == 7)),
                         [w, src[kc]], [p], inc=(kc == 7))
                resid(n, p, G_ap)

    def proj_resid_split(wname, src, G_ap, mid_cb):
        for half in range(2):
            if half == 1:
                mid_cb()
            w = wnext(wname)
            for n in range(8):
                p = ps_get()
                for i in range(4):
                    kc = 4 * half + i
                    S.op("pe", lambda e, n=n, kc=kc, i=i, p=p, w=w: e.matmul(p[:], lhsT=w[:, i * 1024 + n * 128: i * 1024 + (n + 1) * 128], rhs=src[kc][:], start=(i == 0), stop=(i == 3)),
                         [w, src[kc]], [p], inc=(i == 3))
                resid(n, p, G_ap, want_ss=(half == 1))

    def ffn(l, A_ap, B_ap, G_ap, want_ss):
        norm_to_h(A_ap, B_ap)
        for s in range(11):
            w = wnext(f"w_gu{l}")
            first4 = kc_outer(w, [0, 256, 128, 384]) if s == 0 else None
            for q in range(2):
                f = 2 * s + q
                if first4 is not None:
                    pg, pu = first4[2 * q], first4[2 * q + 1]
                else:
                    pg, pu = ps_get(), ps_get()
                    for kc in range(8):
                        S.op("pe", lambda e, w=w, q=q, kc=kc, pg=pg: e.matmul(pg[:], lhsT=w[:, kc * 512 + q * 128: kc * 512 + (q + 1) * 128], rhs=hT[kc][:], start=(kc == 0), stop=(kc == 7)), [w, hT[kc]], [pg], inc=(kc == 7))
                    for kc in range(8):
                        S.op("pe", lambda e, w=w, q=q, kc=kc, pu=pu: e.matmul(pu[:], lhsT=w[:, kc * 512 + (2 + q) * 128: kc * 512 + (3 + q) * 128], rhs=hT[kc][:], start=(kc == 0), stop=(kc == 7)), [w, hT[kc]], [pu], inc=(kc == 7))
                sg = f32_get()
                S.op("act", lambda e, pg=pg, sg=sg: e.activation(out=sg[:], in_=pg[:], func=AF.Silu), [pg], [sg])
                S.op("dve", lambda e, f=f, pu=pu, sg=sg: e.tensor_tensor(out=actT[f][:], in0=sg[:], in1=pu[:], op=ALU.mult), [sg, pu], [actT[f]])
        for n in range(8):
            w = wnext(f"w_dn{l}")
            p = ps_get()
            for f in range(NF):
                S.op("pe", lambda e, w=w, f=f, p=p: e.matmul(p[:], lhsT=w[:, f * 128:(f + 1) * 128], rhs=actT[f][:], start=(f == 0), stop=(f == NF - 1)), [w, actT[f]], [p], inc=(f == NF - 1))
            resid(n, p, G_ap, want_ss)

    xpre = {"done": -1}

    def x_dma(ti, tb):
        t0 = ti * T
        xi = xin[tb % 2]
        S.dma("sp", xi[:], x_d[t0 + tb * 128: t0 + (tb + 1) * 128, :], writes=[xi])

    def prefetch_x(ti):
        if ti < ntiles:
            x_dma(ti, 0)
            x_dma(ti, 1)
            xpre["done"] = ti

    def load_x(ti):
        for tb in range(4):
            xi = xin[tb % 2]
            if not (xpre["done"] == ti and tb < 2):
                x_dma(ti, tb)
            for c in range(7):
                S.op("pe", lambda e, xi=xi, c=c, tb=tb: e.transpose(out=PS[c][:, tb * 128:(tb + 1) * 128], in_=xi[:, c * 128:(c + 1) * 128], identity=ident[:]), [xi, ident], [PS[c]])
            S.op("pe", lambda e, xi=xi, tb=tb: e.transpose(out=SS[:, tb * 128:(tb + 1) * 128], in_=xi[:, 7 * 128:8 * 128], identity=ident[:]), [xi, ident], [SS])
        S.op("dve", lambda e: e.tensor_copy(out=xT[7][:], in_=SS[:]), [SS], [xT[7]])
        for c in range(7):
            if c % 2 == 0:
                S.op("act", lambda e, c=c: e.activation(out=xT[c][:], in_=PS[c][:], func=AF.Copy), [PS[c]], [xT[c]])
            else:
                S.op("dve", lambda e, c=c: e.tensor_copy(out=xT[c][:], in_=PS[c][:]), [PS[c]], [xT[c]])
        for c in range(8):
            sumsq(c)

    def store_x(ti):
        t0 = ti * T
        for tb in range(4):
            for hf in range(2):
                p = ps_get()
                for q in range(4):
                    c = 4 * hf + q
                    S.op("pe", lambda e, c=c, q=q, p=p, tb=tb: e.transpose(out=p[:, q * 128:(q + 1) * 128], in_=xT[c][:, tb * 128:(tb + 1) * 128], identity=ident[:]), [xT[c], ident], [p], inc=(q == 3))
                xo = f32_get()
                if hf == 0:
                    S.op("act", lambda e, p=p, xo=xo: e.activation(out=xo[:], in_=p[:], func=AF.Copy), [p], [xo])
                else:
                    S.op("dve", lambda e, p=p, xo=xo: e.tensor_copy(out=xo[:], in_=p[:]), [p], [xo])
                S.dma("sp", y_d[t0 + tb * 128: t0 + (tb + 1) * 128, hf * 512:(hf + 1) * 512], xo[:], reads=[xo])

    win = {"w": None}

    def win_block(b):
        if b % 4 == 0:
            win["w"] = wnext("w_in")
        return win["w"], (b % 4) * 128

    def layer0(ti):
        A_m, B_m, G_m = der[:, 0:8], mod[0][:, 0:8], mod[0][:, 16:24]
        A_f, B_f, G_f = der[:, 8:16], mod[0][:, 24:32], mod[0][:, 40:48]
        norm_to_h(A_m, B_m)
        stA = [dict() for _ in range(4)]

        def phaseA(j):
            def grp(q):
                w, off = win_block(j * 5 + q)
                p = ps_get()
                for kc in range(8):
                    S.op("pe", lambda e, kc=kc, p=p: e.matmul(p[:], lhsT=w[:, kc * 512 + off: kc * 512 + off + 128], rhs=hT[kc][:], start=(kc == 0), stop=(kc == 7)), [w, hT[kc]], [p], inc=(kc == 7))
                return p
            ggt, xr = GG[j], XR[j]
            if j == 0:
                w0, _ = win_block(0)
                pre = kc_outer(w0, [0, 128, 256, 384])
            else:
                pre = None
            p_ab = pre[0] if pre else grp(0)
            ab = f32_get()
            S.op("act", lambda e: e.activation(out=ab[:], in_=p_ab[:], func=AF.Copy), [p_ab], [ab])
            p_ac = pre[1] if pre else grp(1)
            ac = f32_get()
            S.op("act", lambda e: e.activation(out=ac[:], in_=p_ac[:], func=AF.Copy), [p_ac], [ac])
            p_ax = pre[2] if pre else grp(2)
            S.op("dve", lambda e: e.tensor_tensor(out=Pt[j][:, 2:2 + T], in0=ac[:], in1=p_ax[:], op=ALU.mult), [ac, p_ax], [Pt[j]])
            p_rg = pre[3] if pre else grp(3)
            S.op("act", lambda e: e.activation(out=ggt[:], in_=p_rg[:], func=AF.Gelu_apprx_tanh), [p_rg], [ggt])
            p_rx = grp(4)
            S.op("act", lambda e: e.activation(out=RX[j][:, 3:3 + T], in_=p_rx[:], func=AF.Copy), [p_rx], [RX[j]])
            ca0, ca1 = f32_get(), f32_get()
            wa = lambda k: vec0[:, 16 + j * 3 + k: 17 + j * 3 + k]
            S.op("dve", lambda e: e.tensor_scalar(out=ca0[:], in0=Pt[j][:, 0:T], scalar1=wa(0), scalar2=None, op0=ALU.mult), [Pt[j], vec0], [ca0])
            S.op("dve", lambda e: e.scalar_tensor_tensor(out=ca1[:], in0=Pt[j][:, 1:1 + T], scalar=wa(1), in1=ca0[:], op0=ALU.mult, op1=ALU.add), [Pt[j], vec0, ca0], [ca1])
            S.op("dve", lambda e: e.scalar_tensor_tensor(out=ca0[:], in0=Pt[j][:, 2:2 + T], scalar=wa(2), in1=ca1[:], op0=ALU.mult, op1=ALU.add), [Pt[j], vec0, ca1], [ca0])
            S.op("pool", lambda e: e.tensor_tensor(out=yT[j][:], in0=ab[:], in1=ca0[:], op=ALU.mult), [ab, ca0], [yT[j]])
            xr0, xr1 = f32_get(), f32_get()
            wb = lambda k: vec0[:, 28 + j * 4 + k: 29 + j * 4 + k]
            S.op("dve", lambda e: e.tensor_scalar(out=xr0[:], in0=RX[j][:, 0:T], scalar1=wb(0), scalar2=vec0[:, 44 + j:45 + j], op0=ALU.mult, op1=ALU.add), [RX[j], vec0], [xr0])
            S.op("dve", lambda e: e.scalar_tensor_tensor(out=xr1[:], in0=RX[j][:, 1:1 + T], scalar=wb(1), in1=xr0[:], op0=ALU.mult, op1=ALU.add), [RX[j], vec0, xr0], [xr1])
            S.op("dve", lambda e: e.scalar_tensor_tensor(out=xr0[:], in0=RX[j][:, 2:2 + T], scalar=wb(2), in1=xr1[:], op0=ALU.mult, op1=ALU.add), [RX[j], vec0, xr1], [xr0])
            S.op("dve", lambda e: e.scalar_tensor_tensor(out=xr[:], in0=RX[j][:, 3:3 + T], scalar=wb(3), in1=xr0[:], op0=ALU.mult, op1=ALU.add), [RX[j], vec0, xr0], [xr])
            xrb = bf_get()
            S.op("act", lambda e: e.activation(out=xrb[:], in_=xr[:], func=AF.Copy), [xr], [xrb])
            stA[j]["xrb"] = xrb
            pt_tmp = f32_get()
            S.op("pool", lambda e: e.tensor_copy(out=pt_tmp[:, 0:2], in_=Pt[j][:, T:T + 2]), [Pt[j]], [pt_tmp])
            S.op("pool", lambda e: e.tensor_copy(out=pt_tmp[:, 8:11], in_=RX[j][:, T:T + 3]), [RX[j]], [pt_tmp])
            S.op("pool", lambda e: e.tensor_copy(out=Pt[j][:, 0:2], in_=pt_tmp[:, 0:2]), [pt_tmp], [Pt[j]])
            S.op("pool", lambda e: e.tensor_copy(out=RX[j][:, 0:3], in_=pt_tmp[:, 8:11]), [pt_tmp], [RX[j]])

        def phaseB(js):
            pr = {}
            for j in js:
                xrb = stA[j]["xrb"]
                p_ra, p_ri = ps_get(), ps_get()
                S.op("pe", lambda e, j=j, p_ra=p_ra, xrb=xrb: e.matmul(p_ra[:], lhsT=bdw[:, j, :], rhs=xrb[:], start=True, stop=True), [bdw, xrb], [p_ra])
                S.op("pe", lambda e, j=j, p_ri=p_ri, xrb=xrb: e.matmul(p_ri[:], lhsT=bdw[:, 4 + j, :], rhs=xrb[:], start=True, stop=True), [bdw, xrb], [p_ri])
                pr[j] = (p_ra, p_ri)
            tr, tg, aa, a2, sq = {}, {}, {}, {}, {}
            for j in js:
                tr[j], tg[j] = f32_get(), f32_get()
                p_ra, p_ri = pr[j]
                S.op("act", lambda e, j=j, p_ra=p_ra: e.activation(out=tr[j][:], in_=p_ra[:], func=AF.Tanh, bias=der[:, 48 + j:49 + j], scale=0.5), [p_ra, der], [tr[j]])
                S.op("act", lambda e, j=j, p_ri=p_ri: e.activation(out=tg[j][:], in_=p_ri[:], func=AF.Tanh, bias=der[:, 52 + j:53 + j], scale=0.5), [p_ri, der], [tg[j]])
            for j in js:
                aa[j], a2[j] = f32_get(), f32_get()
                S.op("act", lambda e, j=j: e.activation(out=aa[j][:], in_=tr[j][:], func=AF.Exp, scale=der[:, 56 + j:57 + j], bias=der[:, 56 + j:57 + j]), [tr[j], der], [aa[j]])
                S.op("act", lambda e, j=j: e.activation(out=a2[j][:], in_=tr[j][:], func=AF.Exp, scale=der[:, 32 + j:33 + j], bias=der[:, 32 + j:33 + j]), [tr[j], der], [a2[j]])
            for j in js:
                sq[j] = f32_get()
                S.op("act", lambda e, j=j: e.activation(out=sq[j][:], in_=a2[j][:], func=AF.Sqrt, scale=-0.25, bias=0.25), [a2[j]], [sq[j]])
            for j in js:
                ggt, xr = GG[j], XR[j]
                ix = f32_get()
                S.op("dve", lambda e, j=j, ix=ix, xr=xr: e.scalar_tensor_tensor(out=ix[:], in0=tg[j][:], scalar=1.0, in1=xr[:], op0=ALU.add, op1=ALU.mult), [tg[j], xr], [ix])
                bb = f32_get()
                S.op("dve", lambda e, j=j, ix=ix, bb=bb: e.tensor_tensor(out=bb[:], in0=sq[j][:], in1=ix[:], op=ALU.mult), [sq[j], ix], [bb])
                hs = f32_get()
                S.op("dve", lambda e, j=j, bb=bb, hs=hs: e.tensor_tensor_scan(out=hs[:], data0=aa[j][:], data1=bb[:], initial=hst[:, j:j + 1], op0=ALU.mult, op1=ALU.add), [aa[j], bb, hst], [hs])
                S.op("dve", lambda e, j=j, hs=hs: e.tensor_copy(out=hst[:, j:j + 1], in_=hs[:, T - 1:T]), [hs], [hst])
                S.op("dve", lambda e, j=j, hs=hs, ggt=ggt: e.tensor_tensor(out=yT[4 + j][:], in0=ggt[:], in1=hs[:], op=ALU.mult), [ggt, hs], [yT[4 + j]])

        phaseA(0)
        phaseA(1)
        phaseA(2)
        phaseB((0, 1))
        phaseA(3)
        proj_resid_split("w_out0", yT, G_m, lambda: phaseB((2, 3)))
        ffn(0, A_f, B_f, G_f, do_l1)

    qkn_tog = [0]

    def qkn_a(p):
        sq = bf_get()
        S.op("act", lambda e: e.activation(out=sq[:], in_=p[:], func=AF.Square), [p], [sq])
        return p, sq

    def qkn_b(kf, sq, g_ap, outs):
        qkn_tog[0] ^= 1
        ss = SS if qkn_tog[0] else ps_get()
        S.op("pe", lambda e: e.matmul(ss[:], lhsT=bdones_b[:], rhs=sq[:], start=True, stop=True), [bdones_b, sq], [ss])
        lnv = f32_get()
        S.op("act", lambda e: e.activation(out=lnv[:], in_=ss[:], func=AF.Ln, scale=1.0 / 64, bias=EPS), [ss], [lnv])
        rs = f32_get()
        S.op("act", lambda e: e.activation(out=rs[:], in_=lnv[:], func=AF.Exp, scale=-0.5), [lnv], [rs])
        for (ob, psl) in outs:
            S.op("dve", lambda e, ob=ob, psl=psl: e.scalar_tensor_tensor(out=ob[psl, :], in0=kf[psl, :], scalar=g_ap[psl, :], in1=rs[psl, :], op0=ALU.mult, op1=ALU.mult), [kf, rs, der, vec1], [ob])

    def layer1(ti):
        t0 = ti * T
        A_m, B_m, G_m = der[:, 16:24], mod[1][:, 0:8], mod[1][:, 16:24]
        A_f, B_f, G_f = der[:, 24:32], mod[1][:, 24:32], mod[1][:, 40:48]
        norm_to_h(A_m, B_m)
        pend = []

        def flush_pend(keep):
            while len(pend) > keep:
                fn = pend.pop(0)
                fn()
        for s in range(2):
            w = wnext("w_qkv")
            first4 = kc_outer(w, [0, 128, 256, 384]) if s == 0 else None
            for q in range(4):
                c = 4 * s + q
                if first4 is not None:
                    p = first4[q]
                else:
                    p = ps_get()
                    for kc in range(8):
                        S.op("pe", lambda e, w=w, q=q, kc=kc, p=p: e.matmul(p[:], lhsT=w[:, kc * 512 + q * 128: kc * 512 + (q + 1) * 128], rhs=hT[kc][:], start=(kc == 0), stop=(kc == 7)), [w, hT[kc]], [p], inc=(kc == 7))
                kf, sq = qkn_a(p)

                def fin(c=c, kf=kf, sq=sq):
                    kt = bf_get()
                    qkn_b(kf, sq, vec1[:, 17:18], [(kt, slice(0, 128))])
                    S.dma("sp", kscr[c, :, t0:t0 + T], kt[:], reads=[kt], writes=[kscr_b[c]])
                pend.append(fin)
                flush_pend(1)
        for hv in range(2):
            w = wnext("w_qkv")
            for tb in range(4):
                p = ps_get()
                for kc in range(8):
                    S.op("pe", lambda e, w=w, kc=kc, p=p, tb=tb: e.matmul(p[:], lhsT=hT[kc][:, tb * 128:(tb + 1) * 128], rhs=w[:, kc * 512:(kc + 1) * 512], start=(kc == 0), stop=(kc == 7)), [w, hT[kc]], [p], inc=(kc == 7))
                vt = bf_get()
                if tb % 2 == 0:
                    S.op("act", lambda e, p=p, vt=vt: e.activation(out=vt[:], in_=p[:], func=AF.Copy), [p], [vt])
                else:
                    S.op("dve", lambda e, p=p, vt=vt: e.tensor_copy(out=vt[:], in_=p[:]), [p], [vt])
                S.dma("sp", vscr[4 * hv:4 * hv + 4, :, 4 * ti + tb, :].rearrange("c p f -> p c f"), vt[:].rearrange("p (c f) -> p c f", c=4), reads=[vt], writes=[vscr_b])
                flush_pend(0)
        nkb = 4 * ti + 4
        nkeys = nkb * 128

        def load_kv(c):
            S.dma("sp", KR[c % 2][:, 0:nkeys], kscr[c, :, 0:nkeys], reads=[kscr_b[c]], writes=[KR[c % 2]])
            S.dma("sp", VR[c % 2][:, 0:nkeys], vscr[c, :, 0:nkb, :].rearrange("p k f -> p (k f)"), reads=[vscr_b], writes=[VR[c % 2]])
        wq_first = wnext("w_qkv")
        flush_pend(0)
        if ti > 0:
            load_kv(0)
            load_kv(1)
        for s in range(2):
            w = wq_first if s == 0 else wnext("w_qkv")
            for q in range(4):
                c = 4 * s + q
                p = ps_get()
                for kc in range(8):
                    S.op("pe", lambda e, w=w, q=q, kc=kc, p=p: e.matmul(p[:], lhsT=w[:, kc * 512 + q * 128: kc * 512 + (q + 1) * 128], rhs=hT[kc][:], start=(kc == 0), stop=(kc == 7)), [w, hT[kc]], [p], inc=(kc == 7))
                kf, sq = qkn_a(p)

                def finq(c=c, kf=kf, sq=sq):
                    qkn_b(kf, sq, der[:, 36:37], [(qT[c][0], slice(0, 64)), (qT[c][1], slice(64, 128))])
                pend.append(finq)
                flush_pend(1)
        flush_pend(0)
        if ti == 0:
            load_kv(0)
            load_kv(1)
        Ap = [(P2t[0].rearrange("p (h q) -> p h q", h=2), [PS[0], PS[1]]),
              (P2t[1].rearrange("p (h q) -> p h q", h=2), [PS[2], PS[3]])]
        CSb = [PS[4], PS[5]]
        Ob = [PS[6], PS[7]]
        steps2 = [(c, kb) for c in range(8) for kb in range(nkb - 1, -1, -1)]
        M = len(steps2)
        st = [dict() for _ in range(M)]

        def q0_of(kb):
            return max(0, kb - 4 * ti) * 128

        def QK2(m):
            c, kb = steps2[m]
            q0 = q0_of(kb)
            view, bufs = Ap[m % 2]
            K_ = KR[c % 2]
            for h in range(2):
                diag = kb >= 4 * ti
                S.op("pe", lambda e, h=h: e.matmul(bufs[h][:, q0:T], lhsT=K_[:, kb * 128:(kb + 1) * 128], rhs=qT[c][h][:, q0:T], start=True, stop=False, skip_group_check=True), [K_, qT[c][h]], [bufs[h]], inc=not diag)
                if diag:
                    S.op("pe", lambda e, h=h: e.matmul(bufs[h][:, q0:q0 + 128], lhsT=ident_b[:], rhs=mask_b[:], start=False, stop=False, skip_group_check=True), [ident_b, mask_b], [bufs[h]])

        def EXP2(m):
            c, kb = steps2[m]
            q0 = q0_of(kb)
            view, bufs = Ap[m % 2]
            ev, eb = f32_get2()
            st[m]["ee"] = (ev, eb)
            S.op("act", lambda e: e.activation(out=ev[:, :, q0:T], in_=view[:, :, q0:T], func=AF.Exp), bufs, eb)

        def LN2(m):
            c, kb = steps2[m]
            q0 = q0_of(kb)
            ev, eb = st[m]["ee"]
            sv, sbufs = bf_get2()
            st[m]["sp"] = (sv, sbufs)
            S.op("act", lambda e: e.activation(out=sv[:, :, q0:T], in_=ev[:, :, q0:T], func=AF.Ln, bias=1.0, scale=1.0), eb, sbufs)

        def ACC2(m):
            c, kb = steps2[m]
            q0 = q0_of(kb)
            view, bufs = Ap[m % 2]
            sv, sbufs = st[m]["sp"]
            first = (kb == nkb - 1)
            last = (kb == 0)
            for h in range(2):
                S.op("pe", lambda e, h=h: e.matmul(bufs[h][:, q0:T], lhsT=negL_b[:], rhs=sbufs[h][:, q0:T], start=False, stop=True, skip_group_check=True), [negL_b, sbufs[h]], [bufs[h]], inc=last)
            if not last:
                for h in range(2):
                    S.op("pe", lambda e, h=h: e.matmul(CSb[h][:, q0:T], lhsT=ones_b[:], rhs=sbufs[h][:, q0:T], start=first, stop=False, skip_group_check=True), [ones_b, sbufs[h]], [CSb[h]])

        def RMM2(m):
            c, kb = steps2[m]
            if kb == nkb - 1:
                return
            view, bufs = Ap[m % 2]
            q0p = q0_of(kb + 1)
            for h in range(2):
                S.op("pe", lambda e, h=h: e.matmul(bufs[h][:, q0p:T], lhsT=negones_b[:], rhs=Rb[h][:, q0p:T], start=False, stop=False, skip_group_check=True), [negones_b, Rb[h]], [bufs[h]])

        def RB2(m):
            c, kb = steps2[m]
            q0 = q0_of(kb)
            if kb != 0:
                for h in range(2):
                    S.op("dve", lambda e, h=h: e.tensor_copy(out=Rb[h][0:1, q0:T], in_=CSb[h][0:1, q0:T]), [CSb[h]], [Rb[h]])

        def EXPW2(m):
            c, kb = steps2[m]
            q0 = q0_of(kb)
            view, bufs = Ap[m % 2]
            wv, wbufs = bf_get2()
            st[m]["wt"] = (wv, wbufs)
            S.op("act", lambda e: e.activation(out=wv[:, :, q0:T], in_=view[:, :, q0:T], func=AF.Exp), bufs, wbufs)

        def PV2(m):
            c, kb = steps2[m]
            q0 = q0_of(kb)
            wv, wbufs = st[m]["wt"]
            first = (kb == nkb - 1)
            last = (kb == 0)
            V_ = VR[c % 2]
            for h in range(2):
                S.op("pe", lambda e, h=h: e.matmul(Ob[h][:, q0:T], lhsT=V_[:, kb * 128:(kb + 1) * 128], rhs=wbufs[h][:, q0:T], start=first, stop=last, skip_group_check=True), [V_, wbufs[h]], [Ob[h]])
            if last:
                S.op("dve", lambda e: e.tensor_copy(out=yT[c][0:64, :], in_=Ob[0][0:64, :]), [Ob[0]], [yT[c]])
                S.op("dve", lambda e: e.tensor_copy(out=yT[c][64:128, :], in_=Ob[1][64:128, :]), [Ob[1]], [yT[c]])
                if c + 2 < 8:
                    load_kv(c + 2)

        QK2(0)
        for m in range(M + 2):
            if 0 <= m - 1 < M:
                ACC2(m - 1)
            if m < M:
                EXP2(m)
            if 0 <= m - 1 < M:
                RB2(m - 1)
            if 0 <= m - 2 < M:
                PV2(m - 2)
            if 0 <= m - 1 < M:
                EXPW2(m - 1)
            if m + 1 < M:
                QK2(m + 1)
            if m < M:
                RMM2(m)
                LN2(m)
        proj_resid("w_out1", yT, G_m)
        prefetch_x(ti + 1)
        ffn(1, A_f, B_f, G_f, False)

    for ti in range(ntiles):
        load_x(ti)
        if do_l0:
            layer0(ti)
        if do_l1:
            layer1(ti)
        store_x(ti)
    S.barrier()
    nc._n_inst = S.n_inst
    return nc


def _slabify(W, col_groups):
    KC = W.shape[0] // 128
    out = []
    for cols in col_groups:
        s = W[:, cols].reshape(KC, 128, len(cols)).transpose(1, 0, 2).reshape(128, KC * len(cols))
        out.append(s)
    return np.ascontiguousarray(np.stack(out)).astype(np.float32)


def _vecT(v, n):
    return np.ascontiguousarray(np.asarray(v, np.float32).reshape(n, 128).T)


def _prep_shared(inp):
    ar = np.arange
    sh = {}
    sh["ident"] = np.eye(128, dtype=np.float32)
    j = ar(128)[:, None]
    s = ar(128)[None, :]
    negL = np.where(j >= s, -1.0, 0.0).astype(np.float32)
    mask = np.where(j < s, 1.0, 0.0).astype(np.float32)
    bdones = np.zeros((128, 128), np.float32)
    bdones[:64, :64] = 1.0
    bdones[64:, 64:] = 1.0
    sh["cst"] = np.ascontiguousarray(np.stack([negL, mask, bdones], axis=1))
    bd = np.zeros((128, 8, 128), np.float32)
    for g, key in enumerate(("l0_rg_a_w", "l0_rg_x_w")):
        w = np.asarray(inp[key], np.float32)
        for c in range(4):
            for hh in range(2):
                bd[64 * hh:64 * hh + 64, g * 4 + c, 64 * hh:64 * hh + 64] = w[2 * c + hh]
    sh["bd"] = bd
    v0 = np.zeros((128, 60), np.float32)
    v0[:, 0:8] = _vecT(inp["l0_mix_norm"], 8)
    v0[:, 8:16] = _vecT(inp["l0_ffn_norm"], 8)
    ca = np.asarray(inp["l0_conv_a_w"], np.float32)
    cb = np.asarray(inp["l0_conv_b_w"], np.float32)
    for jj in range(4):
        for k in range(3):
            v0[:, 16 + jj * 3 + k] = ca[k, jj * 128:(jj + 1) * 128]
        for k in range(4):
            v0[:, 28 + jj * 4 + k] = cb[k, jj * 128:(jj + 1) * 128]
    v0[:, 44:48] = _vecT(inp["l0_conv_b_b"], 4)
    v0[:, 48:52] = _vecT(np.asarray(inp["l0_rg_a_b"]).reshape(-1), 4)
    v0[:, 52:56] = _vecT(np.asarray(inp["l0_rg_x_b"]).reshape(-1), 4)
    v0[:, 56:60] = _vecT(inp["l0_rg_lambda"], 4)
    sh["vec0"] = v0
    v1 = np.zeros((128, 18), np.float32)
    v1[:, 0:8] = _vecT(inp["l1_mix_norm"], 8)
    v1[:, 8:16] = _vecT(inp["l1_ffn_norm"], 8)
    v1[:, 16] = np.tile(np.asarray(inp["l1_q_norm"], np.float32), 2)
    v1[:, 17] = np.tile(np.asarray(inp["l1_k_norm"], np.float32), 2)
    sh["vec1"] = v1
    for l in range(2):
        aw = np.asarray(inp[f"l{l}_ada_w"], np.float32)
        sh[f"adaw{l}"] = _slabify(aw, [ar(s * ADA_W, (s + 1) * ADA_W) for s in range(ADA_SL)])
        sh[f"adab{l}"] = _vecT(inp[f"l{l}_ada_b"], 48)
    w_in = np.asarray(inp["l0_w_in"], np.float32)
    blocks = [ar((b % 5) * 512 + (b // 5) * 128, (b % 5) * 512 + (b // 5) * 128 + 128) for b in range(20)]
    sh["w_in"] = _slabify(w_in, [np.concatenate(blocks[4 * s_:4 * s_ + 4]) for s_ in range(5)])
    for l in range(2):
        wo = np.asarray(inp[f"l{l}_w_out"], np.float32)
        if l == 0:
            sh["w_out0"] = np.ascontiguousarray(wo.reshape(2, 4, 128, 1024).transpose(0, 2, 1, 3).reshape(2, 128, 4096))
        else:
            sh["w_out1"] = _slabify(wo, [ar(s * 512, (s + 1) * 512) for s in range(2)])
        wg = np.asarray(inp[f"l{l}_ffn_w_gate"], np.float32)
        wu = np.asarray(inp[f"l{l}_ffn_w_up"], np.float32)
        wgu = np.concatenate([wg, wu], axis=1)
        sh[f"w_gu{l}"] = _slabify(wgu, [np.concatenate([ar(s * 256, (s + 1) * 256), DFF + ar(s * 256, (s + 1) * 256)]) for s in range(11)])
        wd = np.asarray(inp[f"l{l}_ffn_w_down"], np.float32)
        per_oc = _slabify(wd, [ar(n * 128, (n + 1) * 128) for n in range(8)])
        sh[f"w_dn{l}"] = per_oc
    wqkv = np.asarray(inp["l1_w_qkv"], np.float32)
    groups = [ar(1024 + s * 512, 1024 + (s + 1) * 512) for s in range(2)]
    groups += [ar(2048 + s * 512, 2048 + (s + 1) * 512) for s in range(2)]
    groups += [ar(s * 512, (s + 1) * 512) for s in range(2)]
    sh["w_qkv"] = _slabify(wqkv, groups)
    return sh


_NC_CACHE = {}


def _run(inp, do_l0, do_l1, x_full):
    key = (do_l0, do_l1)
    if key not in _NC_CACHE:
        _NC_CACHE[key] = build_nc(do_l0, do_l1)
    nc = _NC_CACHE[key]
    sh = _prep_shared(inp)
    c = np.asarray(inp["c"], np.float32)
    in_maps = []
    for b in range(NCORES):
        m = dict(sh)
        m["x"] = np.ascontiguousarray(x_full[b])
        m["cT"] = _vecT(c[b], 8)
        in_maps.append(m)
    res = run_bass_kernel_spmd(nc, in_maps, core_ids=list(range(NCORES)))
    return np.stack([np.asarray(r["y"]) for r in res.results]).astype(np.float32)


def kernel(**inputs):
    x = np.asarray(inputs["x"], np.float32)
    return _run(inputs, True, True, x)
```

```python
import numpy as np
import concourse.bass as bass
import concourse.mybir as mybir
from concourse.bass_utils import run_bass_kernel_spmd

F32 = mybir.dt.float32
BF16 = mybir.dt.bfloat16
AF = mybir.ActivationFunctionType
ALU = mybir.AluOpType

SEM_ROLL = 30000
NCORES = 8
SEQ = 4096
D = 1024
T = 512
NT = SEQ // T
DFF = 2816
NF = DFF // 128
EPS = 1e-6


class Buf:
    __slots__ = ("ap", "last_w", "readers", "name")

    def __init__(self, ap, name=""):
        self.ap = ap
        self.last_w = None
        self.readers = {}
        self.name = name

    def __getitem__(self, idx):
        return self.ap[idx]


class _Eng:
    def __init__(self, nc, name, eng):
        self.name = name
        self.eng = eng
        self.sem = nc.alloc_semaphore(f"s_{name}_0")
        self.nsem = 1
        self.count = 0
        self.seen = {}
        self.deferred = []


class Sched:
    def __init__(self, nc, n_dma_sems=8):
        self.nc = nc
        self.E = {
            "pe": _Eng(nc, "pe", nc.tensor),
            "act": _Eng(nc, "act", nc.scalar),
            "dve": _Eng(nc, "dve", nc.vector),
            "pool": _Eng(nc, "pool", nc.gpsimd),
            "sp": _Eng(nc, "sp", nc.sync),
        }
        self.dma_sems = {}
        self.n_dma_sems = n_dma_sems
        self.all_tokens = {}
        self.n_inst = 0

    def _deps(self, reads, writes):
        need = {}

        def add(tok):
            if tok is None:
                return
            k = id(tok[0])
            if k not in need or need[k][1] < tok[1]:
                need[k] = tok
        for b in reads:
            add(b.last_w)
        for b in writes:
            add(b.last_w)
            for t in b.readers.values():
                add(t)
        return need

    def _wait(self, E, need, skip_self=False):
        for k, (sem, val) in need.items():
            if skip_self and sem is E.sem:
                continue
            if E.seen.get(k, 0) < val:
                E.eng.wait_ge(sem, val)
                E.seen[k] = val

    def _record(self, tok, reads, writes):
        k = id(tok[0])
        for b in reads:
            b.readers[k] = tok
        for b in writes:
            b.last_w = tok
            b.readers = {}
        self.all_tokens[k] = tok

    def op(self, en, fn, reads=(), writes=(), inc=True):
        E = self.E[en]
        if E.count >= SEM_ROLL:
            E.sem = self.nc.alloc_semaphore(f"s_{E.name}_{E.nsem}")
            E.nsem += 1
            E.count = 0
        need = self._deps(reads, writes)
        self._wait(E, need, skip_self=(en == "pe"))
        ins = fn(E.eng)
        self.n_inst += 1
        if not inc:
            E.deferred.append((tuple(reads), tuple(writes)))
            return ins
        E.count += 1
        ins.then_inc(E.sem, 1)
        tok = (E.sem, E.count)
        for (r_, w_) in E.deferred:
            self._record(tok, r_, w_)
        E.deferred = []
        self._record(tok, reads, writes)
        return ins

    def dma(self, qn, out_ap, in_ap, reads=(), writes=(), **kw):
        E = self.E[qn]
        if qn not in self.dma_sems:
            self.dma_sems[qn] = [[self.nc.alloc_semaphore(f"d_{qn}_{i}"), 0]
                                 for i in range(self.n_dma_sems)]
            self.dma_sems[qn + "_rr"] = 0
        rr = self.dma_sems[qn + "_rr"]
        slot = self.dma_sems[qn][rr % self.n_dma_sems]
        self.dma_sems[qn + "_rr"] = rr + 1
        sem, cnt = slot
        if cnt >= SEM_ROLL:
            if E.seen.get(id(sem), 0) < cnt:
                E.eng.wait_ge(sem, cnt)
            sem = self.nc.alloc_semaphore(f"d_{qn}_r{rr}")
            cnt = 0
            slot[0] = sem
        need = self._deps(reads, writes)
        if cnt > 0:
            k = id(sem)
            if k not in need or need[k][1] < cnt:
                need[k] = (sem, cnt)
        self._wait(E, need)
        ins = E.eng.dma_start(out=out_ap, in_=in_ap, **kw)
        cnt += 16
        slot[1] = cnt
        ins.then_inc(sem, 16)
        self._record((sem, cnt), reads, writes)
        self.n_inst += 1
        return ins

    def barrier(self):
        for E in self.E.values():
            for k, (sem, val) in self.all_tokens.items():
                if E.seen.get(k, 0) < val:
                    E.eng.wait_ge(sem, val)
                    E.seen[k] = val


WSPEC = [
    ("w_in", 5, 8 * 512),
    ("w_out0", 2, 8 * 512),
    ("w_gu0", 11, 8 * 512),
    ("w_dn0", 8, NF * 128),
    ("w_qkv", 6, 8 * 512),
    ("w_out1", 2, 8 * 512),
    ("w_gu1", 11, 8 * 512),
    ("w_dn1", 8, NF * 128),
]
WMAX = 8 * 512
ADA_SL = 12
ADA_W = 512


def build_nc(do_l0=True, do_l1=True, ntiles=NT):
    nc = bass.Bass("TRN2", target_bir_lowering=False)
    S = Sched(nc)

    def din(name, shape, dt=F32):
        return nc.dram_tensor(name, list(shape), dt, kind="ExternalInput").ap()

    x_d = din("x", [SEQ, D])
    y_d = nc.dram_tensor("y", [SEQ, D], F32, kind="ExternalOutput").ap()
    cT_d = din("cT", [128, 8])
    ident_d = din("ident", [128, 128])
    cst_d = din("cst", [128, 3, 128])
    bd_d = din("bd", [128, 8, 128])
    vec0_d = din("vec0", [128, 60])
    vec1_d = din("vec1", [128, 18])
    adaw_d = [din(f"adaw{l}", [ADA_SL, 128, 8 * ADA_W]) for l in range(2)]
    adab_d = [din(f"adab{l}", [128, 48]) for l in range(2)]
    w_d = {n: din(n, [ns, 128, sz]) for (n, ns, sz) in WSPEC}
    ws_d = {n: nc.dram_tensor(n + "_bf", [ns, 128, sz], BF16, kind="Internal").ap() for (n, ns, sz) in WSPEC}
    ws_buf = {n: [Buf(None, f"{n}_s{i}") for i in range(ns)] for (n, ns, sz) in WSPEC}
    kscr = nc.dram_tensor("kscr", [8, 128, SEQ], BF16, kind="Internal").ap()
    vscr = nc.dram_tensor("vscr", [8, 128, SEQ // 128, 128], BF16, kind="Internal").ap()
    kscr_b = [Buf(None, f"kscr{c}") for c in range(8)]
    vscr_b = Buf(None, "vscr")

    def sb(name, shape, dt=F32):
        return Buf(nc.alloc_sbuf_tensor("sb_" + name, list(shape), dt).ap(), name)

    ident = sb("ident", [128, 128])
    ones_b = sb("ones_b", [128, 128], BF16)
    negones_b = sb("negones_b", [128, 128], BF16)
    negL_b = sb("negL_b", [128, 128], BF16)
    mask_b = sb("mask_b", [128, 128], BF16)
    ident_b = sb("ident_b", [128, 128], BF16)
    bdones_b = sb("bdones_b", [128, 128], BF16)
    bdw = sb("bdw", [128, 8, 128], BF16)
    vec0 = sb("vec0", [128, 60])
    vec1 = sb("vec1", [128, 18])
    mod = [sb(f"mod{l}", [128, 48]) for l in range(2)]
    der = sb("der", [128, 64])
    P2t = [nc.alloc_psum_tensor(f"pp{i}", [128, 1024], F32).ap() for i in range(4)]
    PS = [Buf(P2t[i // 2][:, (i % 2) * 512:(i % 2 + 1) * 512], f"ps{i}") for i in range(8)]
    ps_rr = [0]

    def ps_get():
        b = PS[ps_rr[0] % 7]
        ps_rr[0] += 1
        return b
    SS = PS[7]

    NSTG = 3
    with nc.sbuf_tensor("stg32", [128, NSTG, WMAX], F32) as stg32_t, \
            nc.sbuf_tensor("stg16", [128, NSTG, WMAX], BF16) as stg16_t, \
            nc.sbuf_tensor("cst32", [128, 3, 128], F32) as cst32_t, \
            nc.sbuf_tensor("bd32", [128, 8, 128], F32) as bd32_t, \
            nc.sbuf_tensor("ctile", [128, 16], F32) as ctile_t, \
            nc.sbuf_tensor("mrow", [1, 48 * 128], F32) as mrow_t, \
            nc.sbuf_tensor("adab_s", [128, 48], F32) as adab_t:
        stg32 = [Buf(stg32_t.ap()[:, i, :], f"stg32_{i}") for i in range(NSTG)]
        stg16 = [Buf(stg16_t.ap()[:, i, :], f"stg16_{i}") for i in range(NSTG)]
        cst32 = Buf(cst32_t.ap())
        bd32 = Buf(bd32_t.ap())
        ctile = Buf(ctile_t.ap())
        mrow = Buf(mrow_t.ap())
        adab_s = Buf(adab_t.ap())

        S.dma("sp", ident[:], ident_d, writes=[ident])
        S.dma("sp", cst32[:], cst_d, writes=[cst32])
        S.dma("sp", bd32[:], bd_d, writes=[bd32])
        S.dma("sp", vec0[:], vec0_d, writes=[vec0])
        S.dma("sp", vec1[:], vec1_d, writes=[vec1])
        S.dma("sp", ctile[:, 0:8], cT_d, writes=[ctile])
        S.op("pool", lambda e: e.memset(ones_b[:], 1.0), [], [ones_b])
        S.op("pool", lambda e: e.memset(negones_b[:], -1.0), [], [negones_b])
        S.op("dve", lambda e: e.tensor_copy(out=negL_b[:], in_=cst32[:, 0, :]), [cst32], [negL_b])
        S.op("dve", lambda e: e.tensor_scalar(out=mask_b[:], in0=cst32[:, 1, :], scalar1=-1.0, scalar2=240.0, op0=ALU.add, op1=ALU.mult), [cst32], [mask_b])
        S.op("dve", lambda e: e.tensor_copy(out=ident_b[:], in_=ident[:]), [ident], [ident_b])
        S.op("dve", lambda e: e.tensor_copy(out=bdones_b[:], in_=cst32[:, 2, :]), [cst32], [bdones_b])
        S.op("dve", lambda e: e.tensor_copy(out=bdw[:], in_=bd32[:]), [bd32], [bdw])
        S.op("act", lambda e: e.activation(out=ctile[:, 8:16], in_=ctile[:, 0:8], func=AF.Silu), [ctile], [ctile])
        kk = 0
        for l in range(2):
            if (l == 0 and not do_l0) or (l == 1 and not do_l1):
                continue
            for g in range(ADA_SL):
                st = stg32[kk % NSTG]
                kk += 1
                S.dma("sp", st[:, 0:8 * ADA_W], adaw_d[l][g], writes=[st])
                rp = ps_get()
                for kc in range(8):
                    S.op("pe", lambda e, st=st, kc=kc, rp=rp: e.matmul(
                        rp[0:1, 0:ADA_W], lhsT=ctile[:, 8 + kc:9 + kc], rhs=st[:, kc * ADA_W:(kc + 1) * ADA_W],
                        start=(kc == 0), stop=(kc == 7)), [st, ctile], [rp], inc=(kc == 7))
                S.op("dve", lambda e, g=g, rp=rp: e.tensor_copy(out=mrow[0:1, g * ADA_W:(g + 1) * ADA_W], in_=rp[0:1, 0:ADA_W]), [rp], [mrow])
            tp_ = ps_get()
            for j in range(48):
                S.op("pe", lambda e, j=j, tp_=tp_: e.transpose(out=tp_[:, j:j + 1], in_=mrow[0:1, j * 128:(j + 1) * 128], identity=ident[0:1, 0:1]), [mrow, ident], [tp_])
            S.dma("sp", adab_s[:], adab_d[l], writes=[adab_s])
            S.op("dve", lambda e, l=l, tp_=tp_: e.tensor_tensor(out=mod[l][:], in0=tp_[:, 0:48], in1=adab_s[:], op=ALU.add),
                 [tp_, adab_s], [mod[l]])
        S.op("dve", lambda e: e.scalar_tensor_tensor(out=der[:, 0:8], in0=mod[0][:, 8:16], scalar=1.0, in1=vec0[:, 0:8], op0=ALU.add, op1=ALU.mult), [mod[0], vec0], [der])
        S.op("dve", lambda e: e.scalar_tensor_tensor(out=der[:, 8:16], in0=mod[0][:, 32:40], scalar=1.0, in1=vec0[:, 8:16], op0=ALU.add, op1=ALU.mult), [mod[0], vec0], [der])
        S.op("dve", lambda e: e.scalar_tensor_tensor(out=der[:, 16:24], in0=mod[1][:, 8:16], scalar=1.0, in1=vec1[:, 0:8], op0=ALU.add, op1=ALU.mult), [mod[1], vec1], [der])
        S.op("dve", lambda e: e.scalar_tensor_tensor(out=der[:, 24:32], in0=mod[1][:, 32:40], scalar=1.0, in1=vec1[:, 8:16], op0=ALU.add, op1=ALU.mult), [mod[1], vec1], [der])
        S.op("act", lambda e: e.activation(out=der[:, 40:44], in_=vec0[:, 56:60], func=AF.Exp, scale=-1.0), [vec0], [der])
        S.op("act", lambda e: e.activation(out=der[:, 44:48], in_=der[:, 40:44], func=AF.Ln, bias=1.0, scale=1.0), [der], [der])
        S.op("dve", lambda e: e.tensor_scalar(out=der[:, 32:36], in0=der[:, 44:48], scalar1=-8.0, scalar2=None, op0=ALU.mult), [der], [der])
        S.op("dve", lambda e: e.tensor_scalar(out=der[:, 36:37], in0=vec1[:, 16:17], scalar1=0.125, scalar2=None, op0=ALU.mult), [vec1], [der])
        S.op("dve", lambda e: e.tensor_scalar(out=der[:, 48:56], in0=vec0[:, 48:56], scalar1=0.5, scalar2=None, op0=ALU.mult), [vec0], [der])
        S.op("dve", lambda e: e.tensor_scalar(out=der[:, 56:60], in0=der[:, 32:36], scalar1=0.5, scalar2=None, op0=ALU.mult), [der], [der])
        S.barrier()

    xT = [sb(f"xT{c}", [128, T]) for c in range(8)]
    xin = [sb(f"xin{i}", [128, D]) for i in range(2)]
    hT = [sb(f"hT{c}", [128, T], BF16) for c in range(8)]
    yT = [sb(f"yT{c}", [128, T], BF16) for c in range(8)]
    actT = [sb(f"actT{f}", [128, T], BF16) for f in range(NF)]
    qT = [[sb(f"qT{c}_{h}", [128, T], BF16) for h in range(2)] for c in range(8)]
    NFP = 10
    FPt = nc.alloc_sbuf_tensor("sb_fpool", [128, NFP, T], F32).ap()
    FP = [Buf(FPt[:, i, :], f"fp{i}") for i in range(NFP)]
    NBP = 12
    BPt = nc.alloc_sbuf_tensor("sb_bpool", [128, NBP, T], BF16).ap()
    BP = [Buf(BPt[:, i, :], f"bp{i}") for i in range(NBP)]
    rstd = sb("rstd", [128, T])
    GG = [sb(f"ggt{i}", [128, T]) for i in range(4)]
    XR = [sb(f"xr{i}", [128, T]) for i in range(4)]
    WR = [sb(f"wr{i}", [128, WMAX], BF16) for i in range(3)]
    KR = [sb(f"kr{i}", [128, SEQ], BF16) for i in range(2)]
    VR = [sb(f"vr{i}", [128, SEQ], BF16) for i in range(2)]
    Pt = [sb(f"Pt{j}", [128, 2 + T]) for j in range(4)]
    RX = [sb(f"RX{j}", [128, 3 + T]) for j in range(4)]
    hst = sb("hst", [128, 4])
    Rb = [sb(f"Rb{h}", [128, T], BF16) for h in range(2)]
    fp_rr = [0]
    bp_rr = [0]

    def f32_get():
        b = FP[fp_rr[0] % len(FP)]
        fp_rr[0] += 1
        return b

    def bf_get():
        b = BP[bp_rr[0] % len(BP)]
        bp_rr[0] += 1
        return b

    def f32_get2():
        if fp_rr[0] % 2:
            fp_rr[0] += 1
        i = fp_rr[0] % NFP
        fp_rr[0] += 2
        return FPt[:, i:i + 2, :], [FP[i], FP[i + 1]]

    def bf_get2():
        if bp_rr[0] % 2:
            bp_rr[0] += 1
        i = bp_rr[0] % NBP
        bp_rr[0] += 2
        return BPt[:, i:i + 2, :], [BP[i], BP[i + 1]]

    for j in range(4):
        S.op("pool", lambda e, j=j: e.memset(Pt[j][:, 0:2], 0.0), [], [Pt[j]])
        S.op("pool", lambda e, j=j: e.memset(RX[j][:, 0:3], 0.0), [], [RX[j]])
    S.op("pool", lambda e: e.memset(hst[:], 0.0), [], [hst])
    for c in range(8):
        for h in range(2):
            S.op("pool", lambda e, c=c, h=h: e.memset(qT[c][h][:], 0.0), [], [qT[c][h]])
    for h in range(2):
        S.op("pool", lambda e, h=h: e.memset(Rb[h][:], 0.0), [], [Rb[h]])

    seq = []
    for ti in range(ntiles):
        for (n, ns, sz) in WSPEC:
            l1w = n in ("w_qkv", "w_out1", "w_gu1", "w_dn1")
            if (l1w and not do_l1) or ((not l1w) and not do_l0):
                continue
            for s in range(ns):
                seq.append((n, s, sz))
    wst = {"pos": 0, "issued": 0}
    n_tile0 = len(seq) // ntiles
    STG = [KR[0], KR[1], VR[0], VR[1]]
    st_pending = []

    def w_issue(k):
        n, s, sz = seq[k]
        r = WR[k % 3]
        if k < n_tile0:
            h = sz // 2
            for half in range(2):
                stg = STG[(2 * k + half) % 4]
                sv = stg.ap.bitcast(F32)
                S.dma("sp", sv[:, 0:h], w_d[n][s][:, half * h:(half + 1) * h], writes=[stg])
                eng = "pool" if half == 0 else "dve"
                S.op(eng, lambda e, sv=sv, r=r, h=h, half=half: e.tensor_copy(out=r[:, half * h:(half + 1) * h], in_=sv[:, 0:h]), [stg], [r])
            st_pending.append((n, s, sz, r))
        else:
            S.dma("sp", r[:, 0:sz], ws_d[n][s], reads=[ws_buf[n][s]], writes=[r])

    def st_flush(keep):
        while len(st_pending) > keep:
            n, s, sz, r = st_pending.pop(0)
            S.dma("act", ws_d[n][s], r[:, 0:sz], reads=[r], writes=[ws_buf[n][s]])

    def wnext(expect):
        i = wst["pos"]
        wst["pos"] += 1
        assert seq[i][0] == expect, (seq[i], expect)
        while wst["issued"] < min(len(seq), i + 3):
            w_issue(wst["issued"])
            wst["issued"] += 1
            st_flush(1)
        if wst["issued"] >= n_tile0 + 1:
            st_flush(0)
        return WR[i % 3]

    ss_pending = []

    def ss_flush(keep=0):
        while len(ss_pending) > keep:
            n, sq = ss_pending.pop(0)
            S.op("pe", lambda e, n=n, sq=sq: e.matmul(SS[:], lhsT=ones_b[:], rhs=sq[:], start=(n == 0), stop=(n == 7), skip_group_check=True), [ones_b, sq], [SS])

    def sumsq(n):
        sq = bf_get()
        S.op("act", lambda e: e.activation(out=sq[:], in_=xT[n][:], func=AF.Square), [xT[n]], [sq])
        ss_pending.append((n, sq))
        ss_flush(keep=1)

    def resid(n, p, G_ap, want_ss=True):
        S.op("dve", lambda e: e.scalar_tensor_tensor(out=xT[n][:], in0=p[:], scalar=G_ap[:, n:n + 1], in1=xT[n][:], op0=ALU.mult, op1=ALU.add),
             [p, xT[n], mod[0], mod[1]], [xT[n]])
        if want_ss:
            sumsq(n)

    def norm_to_h(A_ap, B_ap):
        ss_flush(0)
        lnv = f32_get()
        S.op("act", lambda e: e.activation(out=lnv[:], in_=SS[:], func=AF.Ln, scale=1.0 / D, bias=EPS), [SS], [lnv])
        S.op("act", lambda e: e.activation(out=rstd[:], in_=lnv[:], func=AF.Exp, scale=-0.5), [lnv], [rstd])
        for c in range(8):
            tmp = f32_get()
            S.op("dve", lambda e, c=c, tmp=tmp: e.scalar_tensor_tensor(out=tmp[:], in0=xT[c][:], scalar=A_ap[:, c:c + 1], in1=rstd[:], op0=ALU.mult, op1=ALU.mult),
                 [xT[c], rstd, der], [tmp])
            S.op("act", lambda e, c=c, tmp=tmp: e.activation(out=hT[c][:], in_=tmp[:], func=AF.Identity, bias=B_ap[:, c:c + 1], scale=1.0), [tmp, mod[0], mod[1]], [hT[c]])

    def kc_outer(w, col_offs):
        banks = [ps_get() for _ in col_offs]
        for kc in range(8):
            for i, off in enumerate(col_offs):
                S.op("pe", lambda e, kc=kc, i=i, off=off: e.matmul(banks[i][:], lhsT=w[:, kc * 512 + off: kc * 512 + off + 128], rhs=hT[kc][:], start=(kc == 0), stop=(kc == 7), skip_group_check=True),
                     [w, hT[kc]], [banks[i]], inc=(kc == 7))
        return banks

    def proj_resid(wname, src, G_ap):
        for s in range(2):
            w = wnext(wname)
            for q in range(4):
                n = 4 * s + q
                p = ps_get()
                for kc in range(8):
                    S.op("pe", lambda e, w=w, q=q, kc=kc, p=p: e.matmul(p[:], lhsT=w[:, kc * 512 + q * 128: kc * 512 + (q + 1) * 128], rhs=src[kc][:], start=(kc == 0), stop=(kc == 7)),
                         [w, src[kc]], [p], inc=(kc == 7))
                resid(n, p, G_ap)

    def proj_resid_split(wname, src, G_ap, mid_cb):
        for half in range(2):
            if half == 1:
                mid_cb()
            w = wnext(wname)
            for n in range(8):
                p = ps_get()
                for i in range(4):
                    kc = 4 * half + i
                    S.op("pe", lambda e, n=n, kc=kc, i=i, p=p, w=w: e.matmul(p[:], lhsT=w[:, i * 1024 + n * 128: i * 1024 + (n + 1) * 128], rhs=src[kc][:], start=(i == 0), stop=(i == 3)),
                         [w, src[kc]], [p], inc=(i == 3))
                resid(n, p, G_ap, want_ss=(half == 1))

    def ffn(l, A_ap, B_ap, G_ap, want_ss):
        norm_to_h(A_ap, B_ap)
        for s in range(11):
            w = wnext(f"w_gu{l}")
            first4 = kc_outer(w, [0, 256, 128, 384]) if s == 0 else None
            for q in range(2):
                f = 2 * s + q
                if first4 is not None:
                    pg, pu = first4[2 * q], first4[2 * q + 1]
                else:
                    pg, pu = ps_get(), ps_get()
                    for kc in range(8):
                        S.op("pe", lambda e, w=w, q=q, kc=kc, pg=pg: e.matmul(pg[:], lhsT=w[:, kc * 512 + q * 128: kc * 512 + (q + 1) * 128], rhs=hT[kc][:], start=(kc == 0), stop=(kc == 7)), [w, hT[kc]], [pg], inc=(kc == 7))
                    for kc in range(8):
                        S.op("pe", lambda e, w=w, q=q, kc=kc, pu=pu: e.matmul(pu[:], lhsT=w[:, kc * 512 + (2 + q) * 128: kc * 512 + (3 + q) * 128], rhs=hT[kc][:], start=(kc == 0), stop=(kc == 7)), [w, hT[kc]], [pu], inc=(kc == 7))
                sg = f32_get()
                S.op("act", lambda e, pg=pg, sg=sg: e.activation(out=sg[:], in_=pg[:], func=AF.Silu), [pg], [sg])
                S.op("dve", lambda e, f=f, pu=pu, sg=sg: e.tensor_tensor(out=actT[f][:], in0=sg[:], in1=pu[:], op=ALU.mult), [sg, pu], [actT[f]])
        for n in range(8):
            w = wnext(f"w_dn{l}")
            p = ps_get()
            for f in range(NF):
                S.op("pe", lambda e, w=w, f=f, p=p: e.matmul(p[:], lhsT=w[:, f * 128:(f + 1) * 128], rhs=actT[f][:], start=(f == 0), stop=(f == NF - 1)), [w, actT[f]], [p], inc=(f == NF - 1))
            resid(n, p, G_ap, want_ss)

    xpre = {"done": -1}

    def x_dma(ti, tb):
        t0 = ti * T
        xi = xin[tb % 2]
        S.dma("sp", xi[:], x_d[t0 + tb * 128: t0 + (tb + 1) * 128, :], writes=[xi])

    def prefetch_x(ti):
        if ti < ntiles:
            x_dma(ti, 0)
            x_dma(ti, 1)
            xpre["done"] = ti

    def load_x(ti):
        for tb in range(4):
            xi = xin[tb % 2]
            if not (xpre["done"] == ti and tb < 2):
                x_dma(ti, tb)
            for c in range(7):
                S.op("pe", lambda e, xi=xi, c=c, tb=tb: e.transpose(out=PS[c][:, tb * 128:(tb + 1) * 128], in_=xi[:, c * 128:(c + 1) * 128], identity=ident[:]), [xi, ident], [PS[c]])
            S.op("pe", lambda e, xi=xi, tb=tb: e.transpose(out=SS[:, tb * 128:(tb + 1) * 128], in_=xi[:, 7 * 128:8 * 128], identity=ident[:]), [xi, ident], [SS])
        S.op("dve", lambda e: e.tensor_copy(out=xT[7][:], in_=SS[:]), [SS], [xT[7]])
        for c in range(7):
            if c % 2 == 0:
                S.op("act", lambda e, c=c: e.activation(out=xT[c][:], in_=PS[c][:], func=AF.Copy), [PS[c]], [xT[c]])
            else:
                S.op("dve", lambda e, c=c: e.tensor_copy(out=xT[c][:], in_=PS[c][:]), [PS[c]], [xT[c]])
        for c in range(8):
            sumsq(c)

    def store_x(ti):
        t0 = ti * T
        for tb in range(4):
            for hf in range(2):
                p = ps_get()
                for q in range(4):
                    c = 4 * hf + q
                    S.op("pe", lambda e, c=c, q=q, p=p, tb=tb: e.transpose(out=p[:, q * 128:(q + 1) * 128], in_=xT[c][:, tb * 128:(tb + 1) * 128], identity=ident[:]), [xT[c], ident], [p], inc=(q == 3))
                xo = f32_get()
                if hf == 0:
                    S.op("act", lambda e, p=p, xo=xo: e.activation(out=xo[:], in_=p[:], func=AF.Copy), [p], [xo])
                else:
                    S.op("dve", lambda e, p=p, xo=xo: e.tensor_copy(out=xo[:], in_=p[:]), [p], [xo])
                S.dma("sp", y_d[t0 + tb * 128: t0 + (tb + 1) * 128, hf * 512:(hf + 1) * 512], xo[:], reads=[xo])

    win = {"w": None}

    def win_block(b):
        if b % 4 == 0:
            win["w"] = wnext("w_in")
        return win["w"], (b % 4) * 128

    def layer0(ti):
        A_m, B_m, G_m = der[:, 0:8], mod[0][:, 0:8], mod[0][:, 16:24]
        A_f, B_f, G_f = der[:, 8:16], mod[0][:, 24:32], mod[0][:, 40:48]
        norm_to_h(A_m, B_m)
        stA = [dict() for _ in range(4)]

        def phaseA(j):
            def grp(q):
                w, off = win_block(j * 5 + q)
                p = ps_get()
                for kc in range(8):
                    S.op("pe", lambda e, kc=kc, p=p: e.matmul(p[:], lhsT=w[:, kc * 512 + off: kc * 512 + off + 128], rhs=hT[kc][:], start=(kc == 0), stop=(kc == 7)), [w, hT[kc]], [p], inc=(kc == 7))
                return p
            ggt, xr = GG[j], XR[j]
            if j == 0:
                w0, _ = win_block(0)
                pre = kc_outer(w0, [0, 128, 256, 384])
            else:
                pre = None
            p_ab = pre[0] if pre else grp(0)
            ab = f32_get()
            S.op("act", lambda e: e.activation(out=ab[:], in_=p_ab[:], func=AF.Copy), [p_ab], [ab])
            p_ac = pre[1] if pre else grp(1)
            ac = f32_get()
            S.op("act", lambda e: e.activation(out=ac[:], in_=p_ac[:], func=AF.Copy), [p_ac], [ac])
            p_ax = pre[2] if pre else grp(2)
            S.op("dve", lambda e: e.tensor_tensor(out=Pt[j][:, 2:2 + T], in0=ac[:], in1=p_ax[:], op=ALU.mult), [ac, p_ax], [Pt[j]])
            p_rg = pre[3] if pre else grp(3)
            S.op("act", lambda e: e.activation(out=ggt[:], in_=p_rg[:], func=AF.Gelu_apprx_tanh), [p_rg], [ggt])
            p_rx = grp(4)
            S.op("act", lambda e: e.activation(out=RX[j][:, 3:3 + T], in_=p_rx[:], func=AF.Copy), [p_rx], [RX[j]])
            ca0, ca1 = f32_get(), f32_get()
            wa = lambda k: vec0[:, 16 + j * 3 + k: 17 + j * 3 + k]
            S.op("dve", lambda e: e.tensor_scalar(out=ca0[:], in0=Pt[j][:, 0:T], scalar1=wa(0), scalar2=None, op0=ALU.mult), [Pt[j], vec0], [ca0])
            S.op("dve", lambda e: e.scalar_tensor_tensor(out=ca1[:], in0=Pt[j][:, 1:1 + T], scalar=wa(1), in1=ca0[:], op0=ALU.mult, op1=ALU.add), [Pt[j], vec0, ca0], [ca1])
            S.op("dve", lambda e: e.scalar_tensor_tensor(out=ca0[:], in0=Pt[j][:, 2:2 + T], scalar=wa(2), in1=ca1[:], op0=ALU.mult, op1=ALU.add), [Pt[j], vec0, ca1], [ca0])
            S.op("dve", lambda e: e.tensor_tensor(out=yT[j][:], in0=ab[:], in1=ca0[:], op=ALU.mult), [ab, ca0], [yT[j]])
            xr0, xr1 = f32_get(), f32_get()
            wb = lambda k: vec0[:, 28 + j * 4 + k: 29 + j * 4 + k]
            S.op("dve", lambda e: e.tensor_scalar(out=xr0[:], in0=RX[j][:, 0:T], scalar1=wb(0), scalar2=vec0[:, 44 + j:45 + j], op0=ALU.mult, op1=ALU.add), [RX[j], vec0], [xr0])
            S.op("dve", lambda e: e.scalar_tensor_tensor(out=xr1[:], in0=RX[j][:, 1:1 + T], scalar=wb(1), in1=xr0[:], op0=ALU.mult, op1=ALU.add), [RX[j], vec0, xr0], [xr1])
            S.op("dve", lambda e: e.scalar_tensor_tensor(out=xr0[:], in0=RX[j][:, 2:2 + T], scalar=wb(2), in1=xr1[:], op0=ALU.mult, op1=ALU.add), [RX[j], vec0, xr1], [xr0])
            S.op("dve", lambda e: e.scalar_tensor_tensor(out=xr[:], in0=RX[j][:, 3:3 + T], scalar=wb(3), in1=xr0[:], op0=ALU.mult, op1=ALU.add), [RX[j], vec0, xr0], [xr])
            xrb = bf_get()
            S.op("act", lambda e: e.activation(out=xrb[:], in_=xr[:], func=AF.Copy), [xr], [xrb])
            stA[j]["xrb"] = xrb
            S.op("act", lambda e: e.activation(out=Pt[j][:, 0:2], in_=Pt[j][:, T:T + 2], func=AF.Copy), [Pt[j]], [Pt[j]])
            S.op("act", lambda e: e.activation(out=RX[j][:, 0:3], in_=RX[j][:, T:T + 3], func=AF.Copy), [RX[j]], [RX[j]])

        def phaseB(js):
            pr = {}
            for j in js:
                xrb = stA[j]["xrb"]
                p_ra, p_ri = ps_get(), ps_get()
                S.op("pe", lambda e, j=j, p_ra=p_ra, xrb=xrb: e.matmul(p_ra[:], lhsT=bdw[:, j, :], rhs=xrb[:], start=True, stop=True), [bdw, xrb], [p_ra])
                S.op("pe", lambda e, j=j, p_ri=p_ri, xrb=xrb: e.matmul(p_ri[:], lhsT=bdw[:, 4 + j, :], rhs=xrb[:], start=True, stop=True), [bdw, xrb], [p_ri])
                pr[j] = (p_ra, p_ri)
            tr, tg, aa, a2, sq = {}, {}, {}, {}, {}
            for j in js:
                tr[j], tg[j] = f32_get(), f32_get()
                p_ra, p_ri = pr[j]
                S.op("act", lambda e, j=j, p_ra=p_ra: e.activation(out=tr[j][:], in_=p_ra[:], func=AF.Tanh, bias=der[:, 48 + j:49 + j], scale=0.5), [p_ra, der], [tr[j]])
                S.op("act", lambda e, j=j, p_ri=p_ri: e.activation(out=tg[j][:], in_=p_ri[:], func=AF.Tanh, bias=der[:, 52 + j:53 + j], scale=0.5), [p_ri, der], [tg[j]])
            for j in js:
                aa[j], a2[j] = f32_get(), f32_get()
                S.op("act", lambda e, j=j: e.activation(out=aa[j][:], in_=tr[j][:], func=AF.Exp, scale=der[:, 56 + j:57 + j], bias=der[:, 56 + j:57 + j]), [tr[j], der], [aa[j]])
                S.op("act", lambda e, j=j: e.activation(out=a2[j][:], in_=tr[j][:], func=AF.Exp, scale=der[:, 32 + j:33 + j], bias=der[:, 32 + j:33 + j]), [tr[j], der], [a2[j]])
            for j in js:
                sq[j] = f32_get()
                S.op("act", lambda e, j=j: e.activation(out=sq[j][:], in_=a2[j][:], func=AF.Sqrt, scale=-0.25, bias=0.25), [a2[j]], [sq[j]])
            for j in js:
                ggt, xr = GG[j], XR[j]
                ix = f32_get()
                S.op("dve", lambda e, j=j, ix=ix, xr=xr: e.scalar_tensor_tensor(out=ix[:], in0=tg[j][:], scalar=1.0, in1=xr[:], op0=ALU.add, op1=ALU.mult), [tg[j], xr], [ix])
                bb = f32_get()
                S.op("dve", lambda e, j=j, ix=ix, bb=bb: e.tensor_tensor(out=bb[:], in0=sq[j][:], in1=ix[:], op=ALU.mult), [sq[j], ix], [bb])
                hs = f32_get()
                S.op("dve", lambda e, j=j, bb=bb, hs=hs: e.tensor_tensor_scan(out=hs[:], data0=aa[j][:], data1=bb[:], initial=hst[:, j:j + 1], op0=ALU.mult, op1=ALU.add), [aa[j], bb, hst], [hs])
                S.op("dve", lambda e, j=j, hs=hs: e.tensor_copy(out=hst[:, j:j + 1], in_=hs[:, T - 1:T]), [hs], [hst])
                S.op("dve", lambda e, j=j, hs=hs, ggt=ggt: e.tensor_tensor(out=yT[4 + j][:], in0=ggt[:], in1=hs[:], op=ALU.mult), [ggt, hs], [yT[4 + j]])

        phaseA(0)
        phaseA(1)
        phaseA(2)
        phaseB((0, 1))
        phaseA(3)
        proj_resid_split("w_out0", yT, G_m, lambda: phaseB((2, 3)))
        ffn(0, A_f, B_f, G_f, do_l1)

    qkn_tog = [0]

    def qkn_a(p):
        sq = bf_get()
        S.op("act", lambda e: e.activation(out=sq[:], in_=p[:], func=AF.Square), [p], [sq])
        return p, sq

    def qkn_b(kf, sq, g_ap, outs):
        qkn_tog[0] ^= 1
        ss = SS if qkn_tog[0] else ps_get()
        S.op("pe", lambda e: e.matmul(ss[:], lhsT=bdones_b[:], rhs=sq[:], start=True, stop=True), [bdones_b, sq], [ss])
        lnv = f32_get()
        S.op("act", lambda e: e.activation(out=lnv[:], in_=ss[:], func=AF.Ln, scale=1.0 / 64, bias=EPS), [ss], [lnv])
        rs = f32_get()
        S.op("act", lambda e: e.activation(out=rs[:], in_=lnv[:], func=AF.Exp, scale=-0.5), [lnv], [rs])
        for (ob, psl) in outs:
            S.op("dve", lambda e, ob=ob, psl=psl: e.scalar_tensor_tensor(out=ob[psl, :], in0=kf[psl, :], scalar=g_ap[psl, :], in1=rs[psl, :], op0=ALU.mult, op1=ALU.mult), [kf, rs, der, vec1], [ob])

    def layer1(ti):
        t0 = ti * T
        A_m, B_m, G_m = der[:, 16:24], mod[1][:, 0:8], mod[1][:, 16:24]
        A_f, B_f, G_f = der[:, 24:32], mod[1][:, 24:32], mod[1][:, 40:48]
        norm_to_h(A_m, B_m)
        pend = []

        def flush_pend(keep):
            while len(pend) > keep:
                fn = pend.pop(0)
                fn()
        for s in range(2):
            w = wnext("w_qkv")
            first4 = kc_outer(w, [0, 128, 256, 384]) if s == 0 else None
            for q in range(4):
                c = 4 * s + q
                if first4 is not None:
                    p = first4[q]
                else:
                    p = ps_get()
                    for kc in range(8):
                        S.op("pe", lambda e, w=w, q=q, kc=kc, p=p: e.matmul(p[:], lhsT=w[:, kc * 512 + q * 128: kc * 512 + (q + 1) * 128], rhs=hT[kc][:], start=(kc == 0), stop=(kc == 7)), [w, hT[kc]], [p], inc=(kc == 7))
                kf, sq = qkn_a(p)

                def fin(c=c, kf=kf, sq=sq):
                    kt = bf_get()
                    qkn_b(kf, sq, vec1[:, 17:18], [(kt, slice(0, 128))])
                    S.dma("sp", kscr[c, :, t0:t0 + T], kt[:], reads=[kt], writes=[kscr_b[c]])
                pend.append(fin)
                flush_pend(2)
        for hv in range(2):
            w = wnext("w_qkv")
            for tb in range(4):
                p = ps_get()
                for kc in range(8):
                    S.op("pe", lambda e, w=w, kc=kc, p=p, tb=tb: e.matmul(p[:], lhsT=hT[kc][:, tb * 128:(tb + 1) * 128], rhs=w[:, kc * 512:(kc + 1) * 512], start=(kc == 0), stop=(kc == 7)), [w, hT[kc]], [p], inc=(kc == 7))
                vt = bf_get()
                if tb % 2 == 0:
                    S.op("act", lambda e, p=p, vt=vt: e.activation(out=vt[:], in_=p[:], func=AF.Copy), [p], [vt])
                else:
                    S.op("dve", lambda e, p=p, vt=vt: e.tensor_copy(out=vt[:], in_=p[:]), [p], [vt])
                S.dma("sp", vscr[4 * hv:4 * hv + 4, :, 4 * ti + tb, :].rearrange("c p f -> p c f"), vt[:].rearrange("p (c f) -> p c f", c=4), reads=[vt], writes=[vscr_b])
                flush_pend(0)
        nkb = 4 * ti + 4
        nkeys = nkb * 128

        def load_kv(c):
            S.dma("sp", KR[c % 2][:, 0:nkeys], kscr[c, :, 0:nkeys], reads=[kscr_b[c]], writes=[KR[c % 2]])
            S.dma("sp", VR[c % 2][:, 0:nkeys], vscr[c, :, 0:nkb, :].rearrange("p k f -> p (k f)"), reads=[vscr_b], writes=[VR[c % 2]])
        wq_first = wnext("w_qkv")
        flush_pend(0)
        if ti > 0:
            load_kv(0)
            load_kv(1)
        for s in range(2):
            w = wq_first if s == 0 else wnext("w_qkv")
            for q in range(4):
                c = 4 * s + q
                p = ps_get()
                for kc in range(8):
                    S.op("pe", lambda e, w=w, q=q, kc=kc, p=p: e.matmul(p[:], lhsT=w[:, kc * 512 + q * 128: kc * 512 + (q + 1) * 128], rhs=hT[kc][:], start=(kc == 0), stop=(kc == 7)), [w, hT[kc]], [p], inc=(kc == 7))
                kf, sq = qkn_a(p)

                def finq(c=c, kf=kf, sq=sq):
                    qkn_b(kf, sq, der[:, 36:37], [(qT[c][0], slice(0, 64)), (qT[c][1], slice(64, 128))])
                pend.append(finq)
                flush_pend(2)
        flush_pend(0)
        if ti == 0:
            load_kv(0)
            load_kv(1)
        Ap = [(P2t[0].rearrange("p (h q) -> p h q", h=2), [PS[0], PS[1]]),
              (P2t[1].rearrange("p (h q) -> p h q", h=2), [PS[2], PS[3]])]
        CSb = [PS[4], PS[5]]
        Ob = [PS[6], PS[7]]
        steps2 = [(c, kb) for c in range(8) for kb in range(nkb - 1, -1, -1)]
        M = len(steps2)
        st = [dict() for _ in range(M)]

        def q0_of(kb):
            return max(0, kb - 4 * ti) * 128

        def QK2(m):
            c, kb = steps2[m]
            q0 = q0_of(kb)
            view, bufs = Ap[m % 2]
            K_ = KR[c % 2]
            for h in range(2):
                diag = kb >= 4 * ti
                S.op("pe", lambda e, h=h: e.matmul(bufs[h][:, q0:T], lhsT=K_[:, kb * 128:(kb + 1) * 128], rhs=qT[c][h][:, q0:T], start=True, stop=False, skip_group_check=True), [K_, qT[c][h]], [bufs[h]], inc=not diag)
                if diag:
                    S.op("pe", lambda e, h=h: e.matmul(bufs[h][:, q0:q0 + 128], lhsT=ident_b[:], rhs=mask_b[:], start=False, stop=False, skip_group_check=True), [ident_b, mask_b], [bufs[h]])

        def EXP2(m):
            c, kb = steps2[m]
            q0 = q0_of(kb)
            view, bufs = Ap[m % 2]
            ev, eb = f32_get2()
            st[m]["ee"] = (ev, eb)
            S.op("act", lambda e: e.activation(out=ev[:, :, q0:T], in_=view[:, :, q0:T], func=AF.Exp), bufs, eb)

        def LN2(m):
            c, kb = steps2[m]
            q0 = q0_of(kb)
            ev, eb = st[m]["ee"]
            sv, sbufs = bf_get2()
            st[m]["sp"] = (sv, sbufs)
            S.op("act", lambda e: e.activation(out=sv[:, :, q0:T], in_=ev[:, :, q0:T], func=AF.Ln, bias=1.0, scale=1.0), eb, sbufs)

        def ACC2(m):
            c, kb = steps2[m]
            q0 = q0_of(kb)
            view, bufs = Ap[m % 2]
            sv, sbufs = st[m]["sp"]
            first = (kb == nkb - 1)
            last = (kb == 0)
            for h in range(2):
                S.op("pe", lambda e, h=h: e.matmul(bufs[h][:, q0:T], lhsT=negL_b[:], rhs=sbufs[h][:, q0:T], start=False, stop=True, skip_group_check=True), [negL_b, sbufs[h]], [bufs[h]], inc=last)
            if not last:
                for h in range(2):
                    S.op("pe", lambda e, h=h: e.matmul(CSb[h][:, q0:T], lhsT=ones_b[:], rhs=sbufs[h][:, q0:T], start=first, stop=False, skip_group_check=True), [ones_b, sbufs[h]], [CSb[h]])

        def RMM2(m):
            c, kb = steps2[m]
            if kb == nkb - 1:
                return
            view, bufs = Ap[m % 2]
            q0p = q0_of(kb + 1)
            for h in range(2):
                S.op("pe", lambda e, h=h: e.matmul(bufs[h][:, q0p:T], lhsT=negones_b[:], rhs=Rb[h][:, q0p:T], start=False, stop=False, skip_group_check=True), [negones_b, Rb[h]], [bufs[h]])

        def RB2(m):
            c, kb = steps2[m]
            q0 = q0_of(kb)
            if kb != 0:
                for h in range(2):
                    S.op("dve", lambda e, h=h: e.tensor_copy(out=Rb[h][0:1, q0:T], in_=CSb[h][0:1, q0:T]), [CSb[h]], [Rb[h]])

        def EXPW2(m):
            c, kb = steps2[m]
            q0 = q0_of(kb)
            view, bufs = Ap[m % 2]
            wv, wbufs = bf_get2()
            st[m]["wt"] = (wv, wbufs)
            S.op("act", lambda e: e.activation(out=wv[:, :, q0:T], in_=view[:, :, q0:T], func=AF.Exp), bufs, wbufs)

        def PV2(m):
            c, kb = steps2[m]
            q0 = q0_of(kb)
            wv, wbufs = st[m]["wt"]
            first = (kb == nkb - 1)
            last = (kb == 0)
            V_ = VR[c % 2]
            for h in range(2):
                S.op("pe", lambda e, h=h: e.matmul(Ob[h][:, q0:T], lhsT=V_[:, kb * 128:(kb + 1) * 128], rhs=wbufs[h][:, q0:T], start=first, stop=last, skip_group_check=True), [V_, wbufs[h]], [Ob[h]])
            if last:
                S.op("dve", lambda e: e.tensor_copy(out=yT[c][0:64, :], in_=Ob[0][0:64, :]), [Ob[0]], [yT[c]])
                S.op("dve", lambda e: e.tensor_copy(out=yT[c][64:128, :], in_=Ob[1][64:128, :]), [Ob[1]], [yT[c]])
                if c + 2 < 8:
                    load_kv(c + 2)

        QK2(0)
        for m in range(M + 2):
            if 0 <= m - 1 < M:
                ACC2(m - 1)
            if m < M:
                EXP2(m)
            if 0 <= m - 1 < M:
                RB2(m - 1)
            if 0 <= m - 2 < M:
                PV2(m - 2)
            if 0 <= m - 1 < M:
                EXPW2(m - 1)
            if m + 1 < M:
                QK2(m + 1)
            if m < M:
                RMM2(m)
                LN2(m)
        proj_resid("w_out1", yT, G_m)
        prefetch_x(ti + 1)
        ffn(1, A_f, B_f, G_f, False)

    for ti in range(ntiles):
        load_x(ti)
        if do_l0:
            layer0(ti)
        if do_l1:
            layer1(ti)
        store_x(ti)
    S.barrier()
    nc._n_inst = S.n_inst
    return nc


def _slabify(W, col_groups):
    KC = W.shape[0] // 128
    out = []
    for cols in col_groups:
        s = W[:, cols].reshape(KC, 128, len(cols)).transpose(1, 0, 2).reshape(128, KC * len(cols))
        out.append(s)
    return np.ascontiguousarray(np.stack(out)).astype(np.float32)


def _vecT(v, n):
    return np.ascontiguousarray(np.asarray(v, np.float32).reshape(n, 128).T)


def _prep_shared(inp):
    ar = np.arange
    sh = {}
    sh["ident"] = np.eye(128, dtype=np.float32)
    j = ar(128)[:, None]
    s = ar(128)[None, :]
    negL = np.where(j >= s, -1.0, 0.0).astype(np.float32)
    mask = np.where(j < s, 1.0, 0.0).astype(np.float32)
    bdones = np.zeros((128, 128), np.float32)
    bdones[:64, :64] = 1.0
    bdones[64:, 64:] = 1.0
    sh["cst"] = np.ascontiguousarray(np.stack([negL, mask, bdones], axis=1))
    bd = np.zeros((128, 8, 128), np.float32)
    for g, key in enumerate(("l0_rg_a_w", "l0_rg_x_w")):
        w = np.asarray(inp[key], np.float32)
        for c in range(4):
            for hh in range(2):
                bd[64 * hh:64 * hh + 64, g * 4 + c, 64 * hh:64 * hh + 64] = w[2 * c + hh]
    sh["bd"] = bd
    v0 = np.zeros((128, 60), np.float32)
    v0[:, 0:8] = _vecT(inp["l0_mix_norm"], 8)
    v0[:, 8:16] = _vecT(inp["l0_ffn_norm"], 8)
    ca = np.asarray(inp["l0_conv_a_w"], np.float32)
    cb = np.asarray(inp["l0_conv_b_w"], np.float32)
    for jj in range(4):
        for k in range(3):
            v0[:, 16 + jj * 3 + k] = ca[k, jj * 128:(jj + 1) * 128]
        for k in range(4):
            v0[:, 28 + jj * 4 + k] = cb[k, jj * 128:(jj + 1) * 128]
    v0[:, 44:48] = _vecT(inp["l0_conv_b_b"], 4)
    v0[:, 48:52] = _vecT(np.asarray(inp["l0_rg_a_b"]).reshape(-1), 4)
    v0[:, 52:56] = _vecT(np.asarray(inp["l0_rg_x_b"]).reshape(-1), 4)
    v0[:, 56:60] = _vecT(inp["l0_rg_lambda"], 4)
    sh["vec0"] = v0
    v1 = np.zeros((128, 18), np.float32)
    v1[:, 0:8] = _vecT(inp["l1_mix_norm"], 8)
    v1[:, 8:16] = _vecT(inp["l1_ffn_norm"], 8)
    v1[:, 16] = np.tile(np.asarray(inp["l1_q_norm"], np.float32), 2)
    v1[:, 17] = np.tile(np.asarray(inp["l1_k_norm"], np.float32), 2)
    sh["vec1"] = v1
    for l in range(2):
        aw = np.asarray(inp[f"l{l}_ada_w"], np.float32)
        sh[f"adaw{l}"] = _slabify(aw, [ar(s * ADA_W, (s + 1) * ADA_W) for s in range(ADA_SL)])
        sh[f"adab{l}"] = _vecT(inp[f"l{l}_ada_b"], 48)
    w_in = np.asarray(inp["l0_w_in"], np.float32)
    blocks = [ar((b % 5) * 512 + (b // 5) * 128, (b % 5) * 512 + (b // 5) * 128 + 128) for b in range(20)]
    sh["w_in"] = _slabify(w_in, [np.concatenate(blocks[4 * s_:4 * s_ + 4]) for s_ in range(5)])
    for l in range(2):
        wo = np.asarray(inp[f"l{l}_w_out"], np.float32)
        if l == 0:
            sh["w_out0"] = np.ascontiguousarray(wo.reshape(2, 4, 128, 1024).transpose(0, 2, 1, 3).reshape(2, 128, 4096))
        else:
            sh["w_out1"] = _slabify(wo, [ar(s * 512, (s + 1) * 512) for s in range(2)])
        wg = np.asarray(inp[f"l{l}_ffn_w_gate"], np.float32)
        wu = np.asarray(inp[f"l{l}_ffn_w_up"], np.float32)
        wgu = np.concatenate([wg, wu], axis=1)
        sh[f"w_gu{l}"] = _slabify(wgu, [np.concatenate([ar(s * 256, (s + 1) * 256), DFF + ar(s * 256, (s + 1) * 256)]) for s in range(11)])
        wd = np.asarray(inp[f"l{l}_ffn_w_down"], np.float32)
        per_oc = _slabify(wd, [ar(n * 128, (n + 1) * 128) for n in range(8)])
        sh[f"w_dn{l}"] = per_oc
    wqkv = np.asarray(inp["l1_w_qkv"], np.float32)
    groups = [ar(1024 + s * 512, 1024 + (s + 1) * 512) for s in range(2)]
    groups += [ar(2048 + s * 512, 2048 + (s + 1) * 512) for s in range(2)]
    groups += [ar(s * 512, (s + 1) * 512) for s in range(2)]
    sh["w_qkv"] = _slabify(wqkv, groups)
    return sh


_NC_CACHE = {}


def _run(inp, do_l0, do_l1, x_full):
    key = (do_l0, do_l1)
    if key not in _NC_CACHE:
        _NC_CACHE[key] = build_nc(do_l0, do_l1)
    nc = _NC_CACHE[key]
    sh = _prep_shared(inp)
    c = np.asarray(inp["c"], np.float32)
    in_maps = []
    for b in range(NCORES):
        m = dict(sh)
        m["x"] = np.ascontiguousarray(x_full[b])
        m["cT"] = _vecT(c[b], 8)
        in_maps.append(m)
    res = run_bass_kernel_spmd(nc, in_maps, core_ids=list(range(NCORES)))
    return np.stack([np.asarray(r["y"]) for r in res.results]).astype(np.float32)


def kernel(**inputs):
    x = np.asarray(inputs["x"], np.float32)
    return _run(inputs, True, True, x)
```

```python
import numpy as np
import concourse.bass as bass
import concourse.mybir as mybir
from concourse.bass_utils import run_bass_kernel_spmd

F32 = mybir.dt.float32
BF16 = mybir.dt.bfloat16
AF = mybir.ActivationFunctionType
ALU = mybir.AluOpType

SEM_ROLL = 30000
NCORES = 8
SEQ = 4096
D = 1024
T = 512
NT = SEQ // T
DFF = 2816
NF = DFF // 128
EPS = 1e-6


class Buf:
    __slots__ = ("ap", "last_w", "readers", "name")

    def __init__(self, ap, name=""):
        self.ap = ap
        self.last_w = None
        self.readers = {}
        self.name = name

    def __getitem__(self, idx):
        return self.ap[idx]


class _Eng:
    def __init__(self, nc, name, eng):
        self.name = name
        self.eng = eng
        self.sem = nc.alloc_semaphore(f"s_{name}_0")
        self.nsem = 1
        self.count = 0
        self.seen = {}
        self.deferred = []


class Sched:
    def __init__(self, nc, n_dma_sems=8):
        self.nc = nc
        self.E = {
            "pe": _Eng(nc, "pe", nc.tensor),
            "act": _Eng(nc, "act", nc.scalar),
            "dve": _Eng(nc, "dve", nc.vector),
            "pool": _Eng(nc, "pool", nc.gpsimd),
            "sp": _Eng(nc, "sp", nc.sync),
        }
        self.dma_sems = {}
        self.n_dma_sems = n_dma_sems
        self.all_tokens = {}
        self.n_inst = 0

    def _deps(self, reads, writes):
        need = {}

        def add(tok):
            if tok is None:
                return
            k = id(tok[0])
            if k not in need or need[k][1] < tok[1]:
                need[k] = tok
        for b in reads:
            add(b.last_w)
        for b in writes:
            add(b.last_w)
            for t in b.readers.values():
                add(t)
        return need

    def _wait(self, E, need, skip_self=False):
        for k, (sem, val) in need.items():
            if skip_self and sem is E.sem:
                continue
            if E.seen.get(k, 0) < val:
                E.eng.wait_ge(sem, val)
                E.seen[k] = val

    def _record(self, tok, reads, writes):
        k = id(tok[0])
        for b in reads:
            b.readers[k] = tok
        for b in writes:
            b.last_w = tok
            b.readers = {}
        self.all_tokens[k] = tok

    def op(self, en, fn, reads=(), writes=(), inc=True):
        E = self.E[en]
        if E.count >= SEM_ROLL:
            E.sem = self.nc.alloc_semaphore(f"s_{E.name}_{E.nsem}")
            E.nsem += 1
            E.count = 0
        need = self._deps(reads, writes)
        self._wait(E, need, skip_self=(en == "pe"))
        ins = fn(E.eng)
        self.n_inst += 1
        if not inc:
            E.deferred.append((tuple(reads), tuple(writes)))
            return ins
        E.count += 1
        ins.then_inc(E.sem, 1)
        tok = (E.sem, E.count)
        for (r_, w_) in E.deferred:
            self._record(tok, r_, w_)
        E.deferred = []
        self._record(tok, reads, writes)
        return ins

    def dma(self, qn, out_ap, in_ap, reads=(), writes=(), **kw):
        E = self.E[qn]
        if qn not in self.dma_sems:
            self.dma_sems[qn] = [[self.nc.alloc_semaphore(f"d_{qn}_{i}"), 0]
                                 for i in range(self.n_dma_sems)]
            self.dma_sems[qn + "_rr"] = 0
        rr = self.dma_sems[qn + "_rr"]
        slot = self.dma_sems[qn][rr % self.n_dma_sems]
        self.dma_sems[qn + "_rr"] = rr + 1
        sem, cnt = slot
        if cnt >= SEM_ROLL:
            if E.seen.get(id(sem), 0) < cnt:
                E.eng.wait_ge(sem, cnt)
            sem = self.nc.alloc_semaphore(f"d_{qn}_r{rr}")
            cnt = 0
            slot[0] = sem
        need = self._deps(reads, writes)
        if cnt > 0:
            k = id(sem)
            if k not in need or need[k][1] < cnt:
                need[k] = (sem, cnt)
        self._wait(E, need)
        ins = E.eng.dma_start(out=out_ap, in_=in_ap, **kw)
        cnt += 16
        slot[1] = cnt
        ins.then_inc(sem, 16)
        self._record((sem, cnt), reads, writes)
        self.n_inst += 1
        return ins

    def barrier(self):
        for E in self.E.values():
            for k, (sem, val) in self.all_tokens.items():
                if E.seen.get(k, 0) < val:
                    E.eng.wait_ge(sem, val)
                    E.seen[k] = val


WSPEC = [
    ("w_in", 5, 8 * 512),
    ("w_out0", 2, 8 * 512),
    ("w_gu0", 11, 8 * 512),
    ("w_dn0", 8, NF * 128),
    ("w_qkv", 6, 8 * 512),
    ("w_out1", 2, 8 * 512),
    ("w_gu1", 11, 8 * 512),
    ("w_dn1", 8, NF * 128),
]
WMAX = 8 * 512
ADA_SL = 12
ADA_W = 512


def build_nc(do_l0=True, do_l1=True, ntiles=NT):
    nc = bass.Bass("TRN2", target_bir_lowering=False)
    S = Sched(nc)

    def din(name, shape, dt=F32):
        return nc.dram_tensor(name, list(shape), dt, kind="ExternalInput").ap()

    x_d = din("x", [SEQ, D])
    y_d = nc.dram_tensor("y", [SEQ, D], F32, kind="ExternalOutput").ap()
    cT_d = din("cT", [128, 8])
    ident_d = din("ident", [128, 128])
    cst_d = din("cst", [128, 3, 128])
    bd_d = din("bd", [128, 8, 128])
    vec0_d = din("vec0", [128, 60])
    vec1_d = din("vec1", [128, 18])
    adaw_d = [din(f"adaw{l}", [ADA_SL, 128, 8 * ADA_W]) for l in range(2)]
    adab_d = [din(f"adab{l}", [128, 48]) for l in range(2)]
    w_d = {n: din(n, [ns, 128, sz]) for (n, ns, sz) in WSPEC}
    ws_d = {n: nc.dram_tensor(n + "_bf", [ns, 128, sz], BF16, kind="Internal").ap() for (n, ns, sz) in WSPEC}
    ws_buf = {n: [Buf(None, f"{n}_s{i}") for i in range(ns)] for (n, ns, sz) in WSPEC}
    kscr = nc.dram_tensor("kscr", [8, 128, SEQ], BF16, kind="Internal").ap()
    vscr = nc.dram_tensor("vscr", [8, 128, SEQ // 128, 128], BF16, kind="Internal").ap()
    kscr_b = [Buf(None, f"kscr{c}") for c in range(8)]
    vscr_b = Buf(None, "vscr")

    def sb(name, shape, dt=F32):
        return Buf(nc.alloc_sbuf_tensor("sb_" + name, list(shape), dt).ap(), name)

    ident = sb("ident", [128, 128])
    ones_b = sb("ones_b", [128, 128], BF16)
    negones_b = sb("negones_b", [128, 128], BF16)
    negL_b = sb("negL_b", [128, 128], BF16)
    mask_b = sb("mask_b", [128, 128], BF16)
    ident_b = sb("ident_b", [128, 128], BF16)
    bdones_b = sb("bdones_b", [128, 128], BF16)
    bdw = sb("bdw", [128, 8, 128], BF16)
    vec0 = sb("vec0", [128, 60])
    vec1 = sb("vec1", [128, 18])
    mod = [sb(f"mod{l}", [128, 48]) for l in range(2)]
    der = sb("der", [128, 64])
    P2t = [nc.alloc_psum_tensor(f"pp{i}", [128, 1024], F32).ap() for i in range(4)]
    PS = [Buf(P2t[i // 2][:, (i % 2) * 512:(i % 2 + 1) * 512], f"ps{i}") for i in range(8)]
    ps_rr = [0]

    def ps_get():
        b = PS[ps_rr[0] % 7]
        ps_rr[0] += 1
        return b
    SS = PS[7]

    NSTG = 3
    with nc.sbuf_tensor("stg32", [128, NSTG, WMAX], F32) as stg32_t, \
            nc.sbuf_tensor("stg16", [128, NSTG, WMAX], BF16) as stg16_t, \
            nc.sbuf_tensor("cst32", [128, 3, 128], F32) as cst32_t, \
            nc.sbuf_tensor("bd32", [128, 8, 128], F32) as bd32_t, \
            nc.sbuf_tensor("ctile", [128, 16], F32) as ctile_t, \
            nc.sbuf_tensor("mrow", [1, 48 * 128], F32) as mrow_t, \
            nc.sbuf_tensor("adab_s", [128, 48], F32) as adab_t:
        stg32 = [Buf(stg32_t.ap()[:, i, :], f"stg32_{i}") for i in range(NSTG)]
        stg16 = [Buf(stg16_t.ap()[:, i, :], f"stg16_{i}") for i in range(NSTG)]
        cst32 = Buf(cst32_t.ap())
        bd32 = Buf(bd32_t.ap())
        ctile = Buf(ctile_t.ap())
        mrow = Buf(mrow_t.ap())
        adab_s = Buf(adab_t.ap())

        S.dma("sp", ident[:], ident_d, writes=[ident])
        S.dma("sp", cst32[:], cst_d, writes=[cst32])
        S.dma("sp", bd32[:], bd_d, writes=[bd32])
        S.dma("sp", vec0[:], vec0_d, writes=[vec0])
        S.dma("sp", vec1[:], vec1_d, writes=[vec1])
        S.dma("sp", ctile[:, 0:8], cT_d, writes=[ctile])
        S.op("pool", lambda e: e.memset(ones_b[:], 1.0), [], [ones_b])
        S.op("pool", lambda e: e.memset(negones_b[:], -1.0), [], [negones_b])
        S.op("dve", lambda e: e.tensor_copy(out=negL_b[:], in_=cst32[:, 0, :]), [cst32], [negL_b])
        S.op("dve", lambda e: e.tensor_scalar(out=mask_b[:], in0=cst32[:, 1, :], scalar1=-1.0, scalar2=240.0, op0=ALU.add, op1=ALU.mult), [cst32], [mask_b])
        S.op("dve", lambda e: e.tensor_copy(out=ident_b[:], in_=ident[:]), [ident], [ident_b])
        S.op("dve", lambda e: e.tensor_copy(out=bdones_b[:], in_=cst32[:, 2, :]), [cst32], [bdones_b])
        S.op("dve", lambda e: e.tensor_copy(out=bdw[:], in_=bd32[:]), [bd32], [bdw])
        S.op("act", lambda e: e.activation(out=ctile[:, 8:16], in_=ctile[:, 0:8], func=AF.Silu), [ctile], [ctile])
        kk = 0
        for l in range(2):
            if (l == 0 and not do_l0) or (l == 1 and not do_l1):
                continue
            for g in range(ADA_SL):
                st = stg32[kk % NSTG]
                kk += 1
                S.dma("sp", st[:, 0:8 * ADA_W], adaw_d[l][g], writes=[st])
                rp = ps_get()
                for kc in range(8):
                    S.op("pe", lambda e, st=st, kc=kc, rp=rp: e.matmul(
                        rp[0:1, 0:ADA_W], lhsT=ctile[:, 8 + kc:9 + kc], rhs=st[:, kc * ADA_W:(kc + 1) * ADA_W],
                        start=(kc == 0), stop=(kc == 7)), [st, ctile], [rp], inc=(kc == 7))
                S.op("dve", lambda e, g=g, rp=rp: e.tensor_copy(out=mrow[0:1, g * ADA_W:(g + 1) * ADA_W], in_=rp[0:1, 0:ADA_W]), [rp], [mrow])
            tp_ = ps_get()
            for j in range(48):
                S.op("pe", lambda e, j=j, tp_=tp_: e.transpose(out=tp_[:, j:j + 1], in_=mrow[0:1, j * 128:(j + 1) * 128], identity=ident[0:1, 0:1]), [mrow, ident], [tp_])
            S.dma("sp", adab_s[:], adab_d[l], writes=[adab_s])
            S.op("dve", lambda e, l=l, tp_=tp_: e.tensor_tensor(out=mod[l][:], in0=tp_[:, 0:48], in1=adab_s[:], op=ALU.add),
                 [tp_, adab_s], [mod[l]])
        S.op("dve", lambda e: e.scalar_tensor_tensor(out=der[:, 0:8], in0=mod[0][:, 8:16], scalar=1.0, in1=vec0[:, 0:8], op0=ALU.add, op1=ALU.mult), [mod[0], vec0], [der])
        S.op("dve", lambda e: e.scalar_tensor_tensor(out=der[:, 8:16], in0=mod[0][:, 32:40], scalar=1.0, in1=vec0[:, 8:16], op0=ALU.add, op1=ALU.mult), [mod[0], vec0], [der])
        S.op("dve", lambda e: e.scalar_tensor_tensor(out=der[:, 16:24], in0=mod[1][:, 8:16], scalar=1.0, in1=vec1[:, 0:8], op0=ALU.add, op1=ALU.mult), [mod[1], vec1], [der])
        S.op("dve", lambda e: e.scalar_tensor_tensor(out=der[:, 24:32], in0=mod[1][:, 32:40], scalar=1.0, in1=vec1[:, 8:16], op0=ALU.add, op1=ALU.mult), [mod[1], vec1], [der])
        S.op("act", lambda e: e.activation(out=der[:, 40:44], in_=vec0[:, 56:60], func=AF.Exp, scale=-1.0), [vec0], [der])
        S.op("act", lambda e: e.activation(out=der[:, 44:48], in_=der[:, 40:44], func=AF.Ln, bias=1.0, scale=1.0), [der], [der])
        S.op("dve", lambda e: e.tensor_scalar(out=der[:, 32:36], in0=der[:, 44:48], scalar1=-8.0, scalar2=None, op0=ALU.mult), [der], [der])
        S.op("dve", lambda e: e.tensor_scalar(out=der[:, 36:37], in0=vec1[:, 16:17], scalar1=0.125, scalar2=None, op0=ALU.mult), [vec1], [der])
        S.op("dve", lambda e: e.tensor_scalar(out=der[:, 48:56], in0=vec0[:, 48:56], scalar1=0.5, scalar2=None, op0=ALU.mult), [vec0], [der])
        S.op("dve", lambda e: e.tensor_scalar(out=der[:, 56:60], in0=der[:, 32:36], scalar1=0.5, scalar2=None, op0=ALU.mult), [der], [der])
        S.barrier()

    xT = [sb(f"xT{c}", [128, T]) for c in range(8)]
    xin = [sb(f"xin{i}", [128, D]) for i in range(2)]
    hT = [sb(f"hT{c}", [128, T], BF16) for c in range(8)]
    yT = [sb(f"yT{c}", [128, T], BF16) for c in range(8)]
    actT = [sb(f"actT{f}", [128, T], BF16) for f in range(NF)]
    qT = [[sb(f"qT{c}_{h}", [128, T], BF16) for h in range(2)] for c in range(8)]
    NFP = 10
    FPt = nc.alloc_sbuf_tensor("sb_fpool", [128, NFP, T], F32).ap()
    FP = [Buf(FPt[:, i, :], f"fp{i}") for i in range(NFP)]
    NBP = 12
    BPt = nc.alloc_sbuf_tensor("sb_bpool", [128, NBP, T], BF16).ap()
    BP = [Buf(BPt[:, i, :], f"bp{i}") for i in range(NBP)]
    rstd = sb("rstd", [128, T])
    GG = [sb(f"ggt{i}", [128, T]) for i in range(4)]
    XR = [sb(f"xr{i}", [128, T]) for i in range(4)]
    WR = [sb(f"wr{i}", [128, WMAX], BF16) for i in range(3)]
    KR = [sb(f"kr{i}", [128, SEQ], BF16) for i in range(2)]
    VR = [sb(f"vr{i}", [128, SEQ], BF16) for i in range(2)]
    Pt = [sb(f"Pt{j}", [128, 2 + T]) for j in range(4)]
    RX = [sb(f"RX{j}", [128, 3 + T]) for j in range(4)]
    hst = sb("hst", [128, 4])
    Rb = [sb(f"Rb{h}", [128, T], BF16) for h in range(2)]
    fp_rr = [0]
    bp_rr = [0]

    def f32_get():
        b = FP[fp_rr[0] % len(FP)]
        fp_rr[0] += 1
        return b

    def bf_get():
        b = BP[bp_rr[0] % len(BP)]
        bp_rr[0] += 1
        return b

    def f32_get2():
        if fp_rr[0] % 2:
            fp_rr[0] += 1
        i = fp_rr[0] % NFP
        fp_rr[0] += 2
        return FPt[:, i:i + 2, :], [FP[i], FP[i + 1]]

    def bf_get2():
        if bp_rr[0] % 2:
            bp_rr[0] += 1
        i = bp_rr[0] % NBP
        bp_rr[0] += 2
        return BPt[:, i:i + 2, :], [BP[i], BP[i + 1]]

    for j in range(4):
        S.op("pool", lambda e, j=j: e.memset(Pt[j][:, 0:2], 0.0), [], [Pt[j]])
        S.op("pool", lambda e, j=j: e.memset(RX[j][:, 0:3], 0.0), [], [RX[j]])
    S.op("pool", lambda e: e.memset(hst[:], 0.0), [], [hst])
    for c in range(8):
        for h in range(2):
            S.op("pool", lambda e, c=c, h=h: e.memset(qT[c][h][:], 0.0), [], [qT[c][h]])
    for h in range(2):
        S.op("pool", lambda e, h=h: e.memset(Rb[h][:], 0.0), [], [Rb[h]])

    seq = []
    for ti in range(ntiles):
        for (n, ns, sz) in WSPEC:
            l1w = n in ("w_qkv", "w_out1", "w_gu1", "w_dn1")
            if (l1w and not do_l1) or ((not l1w) and not do_l0):
                continue
            for s in range(ns):
                seq.append((n, s, sz))
    wst = {"pos": 0, "issued": 0}
    n_tile0 = len(seq) // ntiles
    STG = [KR[0], KR[1], VR[0], VR[1]]
    st_pending = []

    def w_issue(k):
        n, s, sz = seq[k]
        r = WR[k % 3]
        if k < n_tile0:
            h = sz // 2
            for half in range(2):
                stg = STG[(2 * k + half) % 4]
                sv = stg.ap.bitcast(F32)
                S.dma("sp", sv[:, 0:h], w_d[n][s][:, half * h:(half + 1) * h], writes=[stg])
                eng = "pool" if half == 0 else "dve"
                S.op(eng, lambda e, sv=sv, r=r, h=h, half=half: e.tensor_copy(out=r[:, half * h:(half + 1) * h], in_=sv[:, 0:h]), [stg], [r])
            st_pending.append((n, s, sz, r))
        else:
            S.dma("sp", r[:, 0:sz], ws_d[n][s], reads=[ws_buf[n][s]], writes=[r])

    def st_flush(keep):
        while len(st_pending) > keep:
            n, s, sz, r = st_pending.pop(0)
            S.dma("act", ws_d[n][s], r[:, 0:sz], reads=[r], writes=[ws_buf[n][s]])

    def wnext(expect, hold=False):
        i = wst["pos"]
        wst["pos"] += 1
        assert seq[i][0] == expect, (seq[i], expect)
        assert (not hold) or wst["issued"] > i
        while (not hold) and wst["issued"] < min(len(seq), i + 3):
            w_issue(wst["issued"])
            wst["issued"] += 1
            st_flush(1)
        if wst["issued"] >= n_tile0 + 1:
            st_flush(0)
        return WR[i % 3]

    ss_pending = []

    def ss_flush(keep=0):
        while len(ss_pending) > keep:
            n, sq = ss_pending.pop(0)
            S.op("pe", lambda e, n=n, sq=sq: e.matmul(SS[:], lhsT=ones_b[:], rhs=sq[:], start=(n == 0), stop=(n == 7), skip_group_check=True), [ones_b, sq], [SS])

    def sumsq(n):
        sq = bf_get()
        S.op("act", lambda e: e.activation(out=sq[:], in_=xT[n][:], func=AF.Square), [xT[n]], [sq])
        ss_pending.append((n, sq))
        ss_flush(keep=1)

    def resid(n, p, G_ap, want_ss=True):
        S.op("dve", lambda e: e.scalar_tensor_tensor(out=xT[n][:], in0=p[:], scalar=G_ap[:, n:n + 1], in1=xT[n][:], op0=ALU.mult, op1=ALU.add),
             [p, xT[n], mod[0], mod[1]], [xT[n]])
        if want_ss:
            sumsq(n)

    def norm_to_h(A_ap, B_ap):
        ss_flush(0)
        lnv = f32_get()
        S.op("act", lambda e: e.activation(out=lnv[:], in_=SS[:], func=AF.Ln, scale=1.0 / D, bias=EPS), [SS], [lnv])
        S.op("act", lambda e: e.activation(out=rstd[:], in_=lnv[:], func=AF.Exp, scale=-0.5), [lnv], [rstd])
        for c in range(8):
            tmp = f32_get()
            S.op("dve", lambda e, c=c, tmp=tmp: e.scalar_tensor_tensor(out=tmp[:], in0=xT[c][:], scalar=A_ap[:, c:c + 1], in1=rstd[:], op0=ALU.mult, op1=ALU.mult),
                 [xT[c], rstd, der], [tmp])
            S.op("act", lambda e, c=c, tmp=tmp: e.activation(out=hT[c][:], in_=tmp[:], func=AF.Identity, bias=B_ap[:, c:c + 1], scale=1.0), [tmp, mod[0], mod[1]], [hT[c]])

    def kc_outer(w, col_offs):
        banks = [ps_get() for _ in col_offs]
        for kc in range(8):
            for i, off in enumerate(col_offs):
                S.op("pe", lambda e, kc=kc, i=i, off=off: e.matmul(banks[i][:], lhsT=w[:, kc * 512 + off: kc * 512 + off + 128], rhs=hT[kc][:], start=(kc == 0), stop=(kc == 7), skip_group_check=True),
                     [w, hT[kc]], [banks[i]], inc=(kc == 7))
        return banks

    def proj_resid(wname, src, G_ap):
        for s in range(2):
            w = wnext(wname)
            for q in range(4):
                n = 4 * s + q
                p = ps_get()
                for kc in range(8):
                    S.op("pe", lambda e, w=w, q=q, kc=kc, p=p: e.matmul(p[:], lhsT=w[:, kc * 512 + q * 128: kc * 512 + (q + 1) * 128], rhs=src[kc][:], start=(kc == 0), stop=(kc == 7)),
                         [w, src[kc]], [p], inc=(kc == 7))
                resid(n, p, G_ap)

    def proj_resid_split(wname, src, G_ap, mid_cb):
        w0 = wnext(wname)
        for n in range(8):
            p = ps_get()
            for i in range(3):
                S.op("pe", lambda e, n=n, i=i, p=p: e.matmul(p[:], lhsT=w0[:, i * 1024 + n * 128: i * 1024 + (n + 1) * 128], rhs=src[i][:], start=(i == 0), stop=(i == 2)),
                     [w0, src[i]], [p], inc=(i == 2))
            resid(n, p, G_ap, want_ss=False)
        mid_cb()
        w1 = wnext(wname, hold=True)
        for n in range(8):
            p = ps_get()
            S.op("pe", lambda e, n=n, p=p: e.matmul(p[:], lhsT=w0[:, 3 * 1024 + n * 128: 3 * 1024 + (n + 1) * 128], rhs=src[3][:], start=True, stop=False),
                 [w0, src[3]], [p], inc=False)
            for i in range(4):
                S.op("pe", lambda e, n=n, i=i, p=p: e.matmul(p[:], lhsT=w1[:, i * 1024 + n * 128: i * 1024 + (n + 1) * 128], rhs=src[4 + i][:], start=False, stop=(i == 3)),
                     [w1, src[4 + i]], [p], inc=(i == 3))
            resid(n, p, G_ap, want_ss=True)

    def ffn(l, A_ap, B_ap, G_ap, want_ss):
        norm_to_h(A_ap, B_ap)
        for s in range(11):
            w = wnext(f"w_gu{l}")
            first4 = kc_outer(w, [0, 256, 128, 384]) if s == 0 else None
            for q in range(2):
                f = 2 * s + q
                if first4 is not None:
                    pg, pu = first4[2 * q], first4[2 * q + 1]
                else:
                    pg, pu = ps_get(), ps_get()
                    for kc in range(8):
                        S.op("pe", lambda e, w=w, q=q, kc=kc, pg=pg: e.matmul(pg[:], lhsT=w[:, kc * 512 + q * 128: kc * 512 + (q + 1) * 128], rhs=hT[kc][:], start=(kc == 0), stop=(kc == 7)), [w, hT[kc]], [pg], inc=(kc == 7))
                    for kc in range(8):
                        S.op("pe", lambda e, w=w, q=q, kc=kc, pu=pu: e.matmul(pu[:], lhsT=w[:, kc * 512 + (2 + q) * 128: kc * 512 + (3 + q) * 128], rhs=hT[kc][:], start=(kc == 0), stop=(kc == 7)), [w, hT[kc]], [pu], inc=(kc == 7))
                sg = f32_get()
                S.op("act", lambda e, pg=pg, sg=sg: e.activation(out=sg[:], in_=pg[:], func=AF.Silu), [pg], [sg])
                S.op("dve", lambda e, f=f, pu=pu, sg=sg: e.tensor_tensor(out=actT[f][:], in0=sg[:], in1=pu[:], op=ALU.mult), [sg, pu], [actT[f]])
        for n in range(8):
            w = wnext(f"w_dn{l}")
            p = ps_get()
            for f in range(NF):
                S.op("pe", lambda e, w=w, f=f, p=p: e.matmul(p[:], lhsT=w[:, f * 128:(f + 1) * 128], rhs=actT[f][:], start=(f == 0), stop=(f == NF - 1)), [w, actT[f]], [p], inc=(f == NF - 1))
            resid(n, p, G_ap, want_ss)

    xpre = {"done": -1}

    def x_dma(ti, tb):
        t0 = ti * T
        xi = xin[tb % 2]
        S.dma("sp", xi[:], x_d[t0 + tb * 128: t0 + (tb + 1) * 128, :], writes=[xi])

    def prefetch_x(ti):
        if ti < ntiles:
            x_dma(ti, 0)
            x_dma(ti, 1)
            xpre["done"] = ti

    def load_x(ti):
        for tb in range(4):
            xi = xin[tb % 2]
            if not (xpre["done"] == ti and tb < 2):
                x_dma(ti, tb)
            for c in range(7):
                S.op("pe", lambda e, xi=xi, c=c, tb=tb: e.transpose(out=PS[c][:, tb * 128:(tb + 1) * 128], in_=xi[:, c * 128:(c + 1) * 128], identity=ident[:]), [xi, ident], [PS[c]])
            S.op("pe", lambda e, xi=xi, tb=tb: e.transpose(out=SS[:, tb * 128:(tb + 1) * 128], in_=xi[:, 7 * 128:8 * 128], identity=ident[:]), [xi, ident], [SS])
        S.op("dve", lambda e: e.tensor_copy(out=xT[7][:], in_=SS[:]), [SS], [xT[7]])
        for c in range(7):
            if c % 2 == 0:
                S.op("act", lambda e, c=c: e.activation(out=xT[c][:], in_=PS[c][:], func=AF.Copy), [PS[c]], [xT[c]])
            else:
                S.op("dve", lambda e, c=c: e.tensor_copy(out=xT[c][:], in_=PS[c][:]), [PS[c]], [xT[c]])
        for c in range(8):
            sumsq(c)

    def store_x(ti):
        t0 = ti * T
        for tb in range(4):
            for hf in range(2):
                p = ps_get()
                for q in range(4):
                    c = 4 * hf + q
                    S.op("pe", lambda e, c=c, q=q, p=p, tb=tb: e.transpose(out=p[:, q * 128:(q + 1) * 128], in_=xT[c][:, tb * 128:(tb + 1) * 128], identity=ident[:]), [xT[c], ident], [p], inc=(q == 3))
                xo = f32_get()
                if hf == 0:
                    S.op("act", lambda e, p=p, xo=xo: e.activation(out=xo[:], in_=p[:], func=AF.Copy), [p], [xo])
                else:
                    S.op("dve", lambda e, p=p, xo=xo: e.tensor_copy(out=xo[:], in_=p[:]), [p], [xo])
                S.dma("sp", y_d[t0 + tb * 128: t0 + (tb + 1) * 128, hf * 512:(hf + 1) * 512], xo[:], reads=[xo])

    win = {"w": None}

    def win_block(b):
        if b % 4 == 0:
            win["w"] = wnext("w_in")
        return win["w"], (b % 4) * 128

    def layer0(ti):
        A_m, B_m, G_m = der[:, 0:8], mod[0][:, 0:8], mod[0][:, 16:24]
        A_f, B_f, G_f = der[:, 8:16], mod[0][:, 24:32], mod[0][:, 40:48]
        norm_to_h(A_m, B_m)
        stA = [dict() for _ in range(4)]

        def phaseA(j):
            def grp(q):
                w, off = win_block(j * 5 + q)
                p = ps_get()
                for kc in range(8):
                    S.op("pe", lambda e, kc=kc, p=p: e.matmul(p[:], lhsT=w[:, kc * 512 + off: kc * 512 + off + 128], rhs=hT[kc][:], start=(kc == 0), stop=(kc == 7)), [w, hT[kc]], [p], inc=(kc == 7))
                return p
            ggt, xr = GG[j], XR[j]
            if j == 0:
                w0, _ = win_block(0)
                pre = kc_outer(w0, [0, 128, 256, 384])
            else:
                pre = None
            p_ab = pre[0] if pre else grp(0)
            ab = f32_get()
            S.op("act", lambda e: e.activation(out=ab[:], in_=p_ab[:], func=AF.Copy), [p_ab], [ab])
            p_ac = pre[1] if pre else grp(1)
            ac = f32_get()
            S.op("act", lambda e: e.activation(out=ac[:], in_=p_ac[:], func=AF.Copy), [p_ac], [ac])
            p_ax = pre[2] if pre else grp(2)
            S.op("dve", lambda e: e.tensor_tensor(out=Pt[j][:, 2:2 + T], in0=ac[:], in1=p_ax[:], op=ALU.mult), [ac, p_ax], [Pt[j]])
            p_rg = pre[3] if pre else grp(3)
            S.op("act", lambda e: e.activation(out=ggt[:], in_=p_rg[:], func=AF.Gelu_apprx_tanh), [p_rg], [ggt])
            p_rx = grp(4)
            S.op("act", lambda e: e.activation(out=RX[j][:, 3:3 + T], in_=p_rx[:], func=AF.Copy), [p_rx], [RX[j]])
            ca0, ca1 = f32_get(), f32_get()
            wa = lambda k: vec0[:, 16 + j * 3 + k: 17 + j * 3 + k]
            S.op("dve", lambda e: e.tensor_scalar(out=ca0[:], in0=Pt[j][:, 0:T], scalar1=wa(0), scalar2=None, op0=ALU.mult), [Pt[j], vec0], [ca0])
            S.op("dve", lambda e: e.scalar_tensor_tensor(out=ca1[:], in0=Pt[j][:, 1:1 + T], scalar=wa(1), in1=ca0[:], op0=ALU.mult, op1=ALU.add), [Pt[j], vec0, ca0], [ca1])
            S.op("dve", lambda e: e.scalar_tensor_tensor(out=ca0[:], in0=Pt[j][:, 2:2 + T], scalar=wa(2), in1=ca1[:], op0=ALU.mult, op1=ALU.add), [Pt[j], vec0, ca1], [ca0])
            S.op("pool", lambda e: e.tensor_tensor(out=yT[j][:], in0=ab[:], in1=ca0[:], op=ALU.mult), [ab, ca0], [yT[j]])
            xr0, xr1 = f32_get(), f32_get()
            wb = lambda k: vec0[:, 28 + j * 4 + k: 29 + j * 4 + k]
            S.op("dve", lambda e: e.tensor_scalar(out=xr0[:], in0=RX[j][:, 0:T], scalar1=wb(0), scalar2=vec0[:, 44 + j:45 + j], op0=ALU.mult, op1=ALU.add), [RX[j], vec0], [xr0])
            S.op("dve", lambda e: e.scalar_tensor_tensor(out=xr1[:], in0=RX[j][:, 1:1 + T], scalar=wb(1), in1=xr0[:], op0=ALU.mult, op1=ALU.add), [RX[j], vec0, xr0], [xr1])
            S.op("dve", lambda e: e.scalar_tensor_tensor(out=xr0[:], in0=RX[j][:, 2:2 + T], scalar=wb(2), in1=xr1[:], op0=ALU.mult, op1=ALU.add), [RX[j], vec0, xr1], [xr0])
            S.op("dve", lambda e: e.scalar_tensor_tensor(out=xr[:], in0=RX[j][:, 3:3 + T], scalar=wb(3), in1=xr0[:], op0=ALU.mult, op1=ALU.add), [RX[j], vec0, xr0], [xr])
            xrb = bf_get()
            S.op("act", lambda e: e.activation(out=xrb[:], in_=xr[:], func=AF.Copy), [xr], [xrb])
            stA[j]["xrb"] = xrb
            pt_tmp = f32_get()
            S.op("pool", lambda e: e.tensor_copy(out=pt_tmp[:, 0:2], in_=Pt[j][:, T:T + 2]), [Pt[j]], [pt_tmp])
            S.op("pool", lambda e: e.tensor_copy(out=pt_tmp[:, 8:11], in_=RX[j][:, T:T + 3]), [RX[j]], [pt_tmp])
            S.op("pool", lambda e: e.tensor_copy(out=Pt[j][:, 0:2], in_=pt_tmp[:, 0:2]), [pt_tmp], [Pt[j]])
            S.op("pool", lambda e: e.tensor_copy(out=RX[j][:, 0:3], in_=pt_tmp[:, 8:11]), [pt_tmp], [RX[j]])

        def phaseB(js):
            pr = {}
            for j in js:
                xrb = stA[j]["xrb"]
                p_ra, p_ri = ps_get(), ps_get()
                S.op("pe", lambda e, j=j, p_ra=p_ra, xrb=xrb: e.matmul(p_ra[:], lhsT=bdw[:, j, :], rhs=xrb[:], start=True, stop=True), [bdw, xrb], [p_ra])
                S.op("pe", lambda e, j=j, p_ri=p_ri, xrb=xrb: e.matmul(p_ri[:], lhsT=bdw[:, 4 + j, :], rhs=xrb[:], start=True, stop=True), [bdw, xrb], [p_ri])
                pr[j] = (p_ra, p_ri)
            tr, tg, aa, a2, sq = {}, {}, {}, {}, {}
            for j in js:
                tr[j], tg[j] = f32_get(), f32_get()
                p_ra, p_ri = pr[j]
                S.op("act", lambda e, j=j, p_ra=p_ra: e.activation(out=tr[j][:], in_=p_ra[:], func=AF.Tanh, bias=der[:, 48 + j:49 + j], scale=0.5), [p_ra, der], [tr[j]])
                S.op("act", lambda e, j=j, p_ri=p_ri: e.activation(out=tg[j][:], in_=p_ri[:], func=AF.Tanh, bias=der[:, 52 + j:53 + j], scale=0.5), [p_ri, der], [tg[j]])
            for j in js:
                aa[j], a2[j] = f32_get(), f32_get()
                S.op("act", lambda e, j=j: e.activation(out=aa[j][:], in_=tr[j][:], func=AF.Exp, scale=der[:, 56 + j:57 + j], bias=der[:, 56 + j:57 + j]), [tr[j], der], [aa[j]])
                S.op("act", lambda e, j=j: e.activation(out=a2[j][:], in_=tr[j][:], func=AF.Exp, scale=der[:, 32 + j:33 + j], bias=der[:, 32 + j:33 + j]), [tr[j], der], [a2[j]])
            for j in js:
                sq[j] = f32_get()
                S.op("act", lambda e, j=j: e.activation(out=sq[j][:], in_=a2[j][:], func=AF.Sqrt, scale=-0.25, bias=0.25), [a2[j]], [sq[j]])
            for j in js:
                ggt, xr = GG[j], XR[j]
                ix = f32_get()
                S.op("dve", lambda e, j=j, ix=ix, xr=xr: e.scalar_tensor_tensor(out=ix[:], in0=tg[j][:], scalar=1.0, in1=xr[:], op0=ALU.add, op1=ALU.mult), [tg[j], xr], [ix])
                bb = f32_get()
                S.op("dve", lambda e, j=j, ix=ix, bb=bb: e.tensor_tensor(out=bb[:], in0=sq[j][:], in1=ix[:], op=ALU.mult), [sq[j], ix], [bb])
                hs = f32_get()
                S.op("dve", lambda e, j=j, bb=bb, hs=hs: e.tensor_tensor_scan(out=hs[:], data0=aa[j][:], data1=bb[:], initial=hst[:, j:j + 1], op0=ALU.mult, op1=ALU.add), [aa[j], bb, hst], [hs])
                S.op("dve", lambda e, j=j, hs=hs: e.tensor_copy(out=hst[:, j:j + 1], in_=hs[:, T - 1:T]), [hs], [hst])
                S.op("dve", lambda e, j=j, hs=hs, ggt=ggt: e.tensor_tensor(out=yT[4 + j][:], in0=ggt[:], in1=hs[:], op=ALU.mult), [ggt, hs], [yT[4 + j]])

        phaseA(0)
        phaseA(1)
        phaseA(2)
        phaseB((0, 1))
        phaseA(3)
        proj_resid_split("w_out0", yT, G_m, lambda: phaseB((2, 3)))
        ffn(0, A_f, B_f, G_f, do_l1)

    qkn_tog = [0]

    def qkn_a(p):
        sq = bf_get()
        S.op("act", lambda e: e.activation(out=sq[:], in_=p[:], func=AF.Square), [p], [sq])
        return p, sq

    def qkn_b(kf, sq, g_ap, outs):
        qkn_tog[0] ^= 1
        ss = SS if qkn_tog[0] else ps_get()
        S.op("pe", lambda e: e.matmul(ss[:], lhsT=bdones_b[:], rhs=sq[:], start=True, stop=True), [bdones_b, sq], [ss])
        lnv = f32_get()
        S.op("act", lambda e: e.activation(out=lnv[:], in_=ss[:], func=AF.Ln, scale=1.0 / 64, bias=EPS), [ss], [lnv])
        rs = f32_get()
        S.op("act", lambda e: e.activation(out=rs[:], in_=lnv[:], func=AF.Exp, scale=-0.5), [lnv], [rs])
        for (ob, psl) in outs:
            S.op("dve", lambda e, ob=ob, psl=psl: e.scalar_tensor_tensor(out=ob[psl, :], in0=kf[psl, :], scalar=g_ap[psl, :], in1=rs[psl, :], op0=ALU.mult, op1=ALU.mult), [kf, rs, der, vec1], [ob])

    def layer1(ti):
        t0 = ti * T
        A_m, B_m, G_m = der[:, 16:24], mod[1][:, 0:8], mod[1][:, 16:24]
        A_f, B_f, G_f = der[:, 24:32], mod[1][:, 24:32], mod[1][:, 40:48]
        norm_to_h(A_m, B_m)
        pend = []

        def flush_pend(keep):
            while len(pend) > keep:
                fn = pend.pop(0)
                fn()
        for s in range(2):
            w = wnext("w_qkv")
            first4 = kc_outer(w, [0, 128, 256, 384]) if s == 0 else None
            for q in range(4):
                c = 4 * s + q
                if first4 is not None:
                    p = first4[q]
                else:
                    p = ps_get()
                    for kc in range(8):
                        S.op("pe", lambda e, w=w, q=q, kc=kc, p=p: e.matmul(p[:], lhsT=w[:, kc * 512 + q * 128: kc * 512 + (q + 1) * 128], rhs=hT[kc][:], start=(kc == 0), stop=(kc == 7)), [w, hT[kc]], [p], inc=(kc == 7))
                kf, sq = qkn_a(p)

                def fin(c=c, kf=kf, sq=sq):
                    kt = bf_get()
                    qkn_b(kf, sq, vec1[:, 17:18], [(kt, slice(0, 128))])
                    S.dma("sp", kscr[c, :, t0:t0 + T], kt[:], reads=[kt], writes=[kscr_b[c]])
                pend.append(fin)
                flush_pend(2)
        for hv in range(2):
            w = wnext("w_qkv")
            for tb in range(4):
                p = ps_get()
                for kc in range(8):
                    S.op("pe", lambda e, w=w, kc=kc, p=p, tb=tb: e.matmul(p[:], lhsT=hT[kc][:, tb * 128:(tb + 1) * 128], rhs=w[:, kc * 512:(kc + 1) * 512], start=(kc == 0), stop=(kc == 7)), [w, hT[kc]], [p], inc=(kc == 7))
                vt = bf_get()
                if tb % 2 == 0:
                    S.op("act", lambda e, p=p, vt=vt: e.activation(out=vt[:], in_=p[:], func=AF.Copy), [p], [vt])
                else:
                    S.op("dve", lambda e, p=p, vt=vt: e.tensor_copy(out=vt[:], in_=p[:]), [p], [vt])
                S.dma("sp", vscr[4 * hv:4 * hv + 4, :, 4 * ti + tb, :].rearrange("c p f -> p c f"), vt[:].rearrange("p (c f) -> p c f", c=4), reads=[vt], writes=[vscr_b])
                flush_pend(0)
        nkb = 4 * ti + 4
        nkeys = nkb * 128

        def load_kv(c):
            S.dma("sp", KR[c % 2][:, 0:nkeys], kscr[c, :, 0:nkeys], reads=[kscr_b[c]], writes=[KR[c % 2]])
            S.dma("sp", VR[c % 2][:, 0:nkeys], vscr[c, :, 0:nkb, :].rearrange("p k f -> p (k f)"), reads=[vscr_b], writes=[VR[c % 2]])
        wq_first = wnext("w_qkv")
        flush_pend(0)
        if ti > 0:
            load_kv(0)
            load_kv(1)
        for s in range(2):
            w = wq_first if s == 0 else wnext("w_qkv")
            for q in range(4):
                c = 4 * s + q
                p = ps_get()
                for kc in range(8):
                    S.op("pe", lambda e, w=w, q=q, kc=kc, p=p: e.matmul(p[:], lhsT=w[:, kc * 512 + q * 128: kc * 512 + (q + 1) * 128], rhs=hT[kc][:], start=(kc == 0), stop=(kc == 7)), [w, hT[kc]], [p], inc=(kc == 7))
                kf, sq = qkn_a(p)

                def finq(c=c, kf=kf, sq=sq):
                    qkn_b(kf, sq, der[:, 36:37], [(qT[c][0], slice(0, 64)), (qT[c][1], slice(64, 128))])
                pend.append(finq)
                flush_pend(2)
        flush_pend(0)
        if ti == 0:
            load_kv(0)
            load_kv(1)
        Ap = [(P2t[0].rearrange("p (h q) -> p h q", h=2), [PS[0], PS[1]]),
              (P2t[1].rearrange("p (h q) -> p h q", h=2), [PS[2], PS[3]])]
        CSb = [PS[4], PS[5]]
        Ob = [PS[6], PS[7]]
        steps2 = [(c, kb) for c in range(8) for kb in range(nkb - 1, -1, -1)]
        M = len(steps2)
        st = [dict() for _ in range(M)]

        def q0_of(kb):
            return max(0, kb - 4 * ti) * 128

        def QK2(m):
            c, kb = steps2[m]
            q0 = q0_of(kb)
            view, bufs = Ap[m % 2]
            K_ = KR[c % 2]
            for h in range(2):
                diag = kb >= 4 * ti
                S.op("pe", lambda e, h=h: e.matmul(bufs[h][:, q0:T], lhsT=K_[:, kb * 128:(kb + 1) * 128], rhs=qT[c][h][:, q0:T], start=True, stop=False, skip_group_check=True), [K_, qT[c][h]], [bufs[h]], inc=not diag)
                if diag:
                    S.op("pe", lambda e, h=h: e.matmul(bufs[h][:, q0:q0 + 128], lhsT=ident_b[:], rhs=mask_b[:], start=False, stop=False, skip_group_check=True), [ident_b, mask_b], [bufs[h]])

        def EXP2(m):
            c, kb = steps2[m]
            q0 = q0_of(kb)
            view, bufs = Ap[m % 2]
            ev, eb = f32_get2()
            st[m]["ee"] = (ev, eb)
            S.op("act", lambda e: e.activation(out=ev[:, :, q0:T], in_=view[:, :, q0:T], func=AF.Exp), bufs, eb)

        def LN2(m):
            c, kb = steps2[m]
            q0 = q0_of(kb)
            ev, eb = st[m]["ee"]
            sv, sbufs = bf_get2()
            st[m]["sp"] = (sv, sbufs)
            S.op("act", lambda e: e.activation(out=sv[:, :, q0:T], in_=ev[:, :, q0:T], func=AF.Ln, bias=1.0, scale=1.0), eb, sbufs)

        def ACC2(m):
            c, kb = steps2[m]
            q0 = q0_of(kb)
            view, bufs = Ap[m % 2]
            sv, sbufs = st[m]["sp"]
            first = (kb == nkb - 1)
            last = (kb == 0)
            for h in range(2):
                S.op("pe", lambda e, h=h: e.matmul(bufs[h][:, q0:T], lhsT=negL_b[:], rhs=sbufs[h][:, q0:T], start=False, stop=True, skip_group_check=True), [negL_b, sbufs[h]], [bufs[h]], inc=last)
            if not last:
                for h in range(2):
                    S.op("pe", lambda e, h=h: e.matmul(CSb[h][:, q0:T], lhsT=ones_b[:], rhs=sbufs[h][:, q0:T], start=first, stop=False, skip_group_check=True), [ones_b, sbufs[h]], [CSb[h]])

        def RMM2(m):
            c, kb = steps2[m]
            if kb == nkb - 1:
                return
            view, bufs = Ap[m % 2]
            q0p = q0_of(kb + 1)
            for h in range(2):
                S.op("pe", lambda e, h=h: e.matmul(bufs[h][:, q0p:T], lhsT=negones_b[:], rhs=Rb[h][:, q0p:T], start=False, stop=False, skip_group_check=True), [negones_b, Rb[h]], [bufs[h]])

        def RB2(m):
            c, kb = steps2[m]
            q0 = q0_of(kb)
            if kb != 0:
                for h in range(2):
                    S.op("dve", lambda e, h=h: e.tensor_copy(out=Rb[h][0:1, q0:T], in_=CSb[h][0:1, q0:T]), [CSb[h]], [Rb[h]])

        def EXPW2(m):
            c, kb = steps2[m]
            q0 = q0_of(kb)
            view, bufs = Ap[m % 2]
            wv, wbufs = bf_get2()
            st[m]["wt"] = (wv, wbufs)
            S.op("act", lambda e: e.activation(out=wv[:, :, q0:T], in_=view[:, :, q0:T], func=AF.Exp), bufs, wbufs)

        def PV2(m):
            c, kb = steps2[m]
            q0 = q0_of(kb)
            wv, wbufs = st[m]["wt"]
            first = (kb == nkb - 1)
            last = (kb == 0)
            V_ = VR[c % 2]
            for h in range(2):
                S.op("pe", lambda e, h=h: e.matmul(Ob[h][:, q0:T], lhsT=V_[:, kb * 128:(kb + 1) * 128], rhs=wbufs[h][:, q0:T], start=first, stop=last, skip_group_check=True), [V_, wbufs[h]], [Ob[h]])
            if last:
                S.op("dve", lambda e: e.tensor_copy(out=yT[c][0:64, :], in_=Ob[0][0:64, :]), [Ob[0]], [yT[c]])
                S.op("dve", lambda e: e.tensor_copy(out=yT[c][64:128, :], in_=Ob[1][64:128, :]), [Ob[1]], [yT[c]])
                if c + 2 < 8:
                    load_kv(c + 2)

        QK2(0)
        for m in range(M + 2):
            if 0 <= m - 1 < M:
                ACC2(m - 1)
            if m < M:
                EXP2(m)
            if 0 <= m - 1 < M:
                RB2(m - 1)
            if 0 <= m - 2 < M:
                PV2(m - 2)
            if 0 <= m - 1 < M:
                EXPW2(m - 1)
            if m + 1 < M:
                QK2(m + 1)
            if m < M:
                RMM2(m)
                LN2(m)
        proj_resid("w_out1", yT, G_m)
        prefetch_x(ti + 1)
        ffn(1, A_f, B_f, G_f, False)

    for ti in range(ntiles):
        load_x(ti)
        if do_l0:
            layer0(ti)
        if do_l1:
            layer1(ti)
        store_x(ti)
    S.barrier()
    nc._n_inst = S.n_inst
    return nc


def _slabify(W, col_groups):
    KC = W.shape[0] // 128
    out = []
    for cols in col_groups:
        s = W[:, cols].reshape(KC, 128, len(cols)).transpose(1, 0, 2).reshape(128, KC * len(cols))
        out.append(s)
    return np.ascontiguousarray(np.stack(out)).astype(np.float32)


def _vecT(v, n):
    return np.ascontiguousarray(np.asarray(v, np.float32).reshape(n, 128).T)


def _prep_shared(inp):
    ar = np.arange
    sh = {}
    sh["ident"] = np.eye(128, dtype=np.float32)
    j = ar(128)[:, None]
    s = ar(128)[None, :]
    negL = np.where(j >= s, -1.0, 0.0).astype(np.float32)
    mask = np.where(j < s, 1.0, 0.0).astype(np.float32)
    bdones = np.zeros((128, 128), np.float32)
    bdones[:64, :64] = 1.0
    bdones[64:, 64:] = 1.0
    sh["cst"] = np.ascontiguousarray(np.stack([negL, mask, bdones], axis=1))
    bd = np.zeros((128, 8, 128), np.float32)
    for g, key in enumerate(("l0_rg_a_w", "l0_rg_x_w")):
        w = np.asarray(inp[key], np.float32)
        for c in range(4):
            for hh in range(2):
                bd[64 * hh:64 * hh + 64, g * 4 + c, 64 * hh:64 * hh + 64] = w[2 * c + hh]
    sh["bd"] = bd
    v0 = np.zeros((128, 60), np.float32)
    v0[:, 0:8] = _vecT(inp["l0_mix_norm"], 8)
    v0[:, 8:16] = _vecT(inp["l0_ffn_norm"], 8)
    ca = np.asarray(inp["l0_conv_a_w"], np.float32)
    cb = np.asarray(inp["l0_conv_b_w"], np.float32)
    for jj in range(4):
        for k in range(3):
            v0[:, 16 + jj * 3 + k] = ca[k, jj * 128:(jj + 1) * 128]
        for k in range(4):
            v0[:, 28 + jj * 4 + k] = cb[k, jj * 128:(jj + 1) * 128]
    v0[:, 44:48] = _vecT(inp["l0_conv_b_b"], 4)
    v0[:, 48:52] = _vecT(np.asarray(inp["l0_rg_a_b"]).reshape(-1), 4)
    v0[:, 52:56] = _vecT(np.asarray(inp["l0_rg_x_b"]).reshape(-1), 4)
    v0[:, 56:60] = _vecT(inp["l0_rg_lambda"], 4)
    sh["vec0"] = v0
    v1 = np.zeros((128, 18), np.float32)
    v1[:, 0:8] = _vecT(inp["l1_mix_norm"], 8)
    v1[:, 8:16] = _vecT(inp["l1_ffn_norm"], 8)
    v1[:, 16] = np.tile(np.asarray(inp["l1_q_norm"], np.float32), 2)
    v1[:, 17] = np.tile(np.asarray(inp["l1_k_norm"], np.float32), 2)
    sh["vec1"] = v1
    for l in range(2):
        aw = np.asarray(inp[f"l{l}_ada_w"], np.float32)
        sh[f"adaw{l}"] = _slabify(aw, [ar(s * ADA_W, (s + 1) * ADA_W) for s in range(ADA_SL)])
        sh[f"adab{l}"] = _vecT(inp[f"l{l}_ada_b"], 48)
    w_in = np.asarray(inp["l0_w_in"], np.float32)
    blocks = [ar((b % 5) * 512 + (b // 5) * 128, (b % 5) * 512 + (b // 5) * 128 + 128) for b in range(20)]
    sh["w_in"] = _slabify(w_in, [np.concatenate(blocks[4 * s_:4 * s_ + 4]) for s_ in range(5)])
    for l in range(2):
        wo = np.asarray(inp[f"l{l}_w_out"], np.float32)
        if l == 0:
            sh["w_out0"] = np.ascontiguousarray(wo.reshape(2, 4, 128, 1024).transpose(0, 2, 1, 3).reshape(2, 128, 4096))
        else:
            sh["w_out1"] = _slabify(wo, [ar(s * 512, (s + 1) * 512) for s in range(2)])
        wg = np.asarray(inp[f"l{l}_ffn_w_gate"], np.float32)
        wu = np.asarray(inp[f"l{l}_ffn_w_up"], np.float32)
        wgu = np.concatenate([wg, wu], axis=1)
        sh[f"w_gu{l}"] = _slabify(wgu, [np.concatenate([ar(s * 256, (s + 1) * 256), DFF + ar(s * 256, (s + 1) * 256)]) for s in range(11)])
        wd = np.asarray(inp[f"l{l}_ffn_w_down"], np.float32)
        per_oc = _slabify(wd, [ar(n * 128, (n + 1) * 128) for n in range(8)])
        sh[f"w_dn{l}"] = per_oc
    wqkv = np.asarray(inp["l1_w_qkv"], np.float32)
    groups = [ar(1024 + s * 512, 1024 + (s + 1) * 512) for s in range(2)]
    groups += [ar(2048 + s * 512, 2048 + (s + 1) * 512) for s in range(2)]
    groups += [ar(s * 512, (s + 1) * 512) for s in range(2)]
    sh["w_qkv"] = _slabify(wqkv, groups)
    return sh


_NC_CACHE = {}


def _run(inp, do_l0, do_l1, x_full):
    key = (do_l0, do_l1)
    if key not in _NC_CACHE:
        _NC_CACHE[key] = build_nc(do_l0, do_l1)
    nc = _NC_CACHE[key]
    sh = _prep_shared(inp)
    c = np.asarray(inp["c"], np.float32)
    in_maps = []
    for b in range(NCORES):
        m = dict(sh)
        m["x"] = np.ascontiguousarray(x_full[b])
        m["cT"] = _vecT(c[b], 8)
        in_maps.append(m)
    res = run_bass_kernel_spmd(nc, in_maps, core_ids=list(range(NCORES)))
    return np.stack([np.asarray(r["y"]) for r in res.results]).astype(np.float32)


def kernel(**inputs):
    x = np.asarray(inputs["x"], np.float32)
    return _run(inputs, True, True, x)
```
